# Optimizing a Trainium2 kernel written in Bass

```python
import jax, jax.numpy as jnp
from jax import lax
import numpy as np

D_MODEL = 4096
BATCH = 1
SEQ = 16384
DEPTH = 2

ATTN_HEADS = 16
ATTN_HEAD_DIM = 128
ATTN_WIDTH = ATTN_HEADS * ATTN_HEAD_DIM
Q_BLOCK = 128
CONV_CHANNELS = 1024
CONV_WIDTH = 31
HGRN_HEADS = 8
HGRN_HEAD_DIM = 128
HGRN_WIDTH = HGRN_HEADS * HGRN_HEAD_DIM
HGRN_CHUNK = 64
D_FF = 11008
FFN_CONV_WIDTH = 3
IN_SPLITS = (ATTN_WIDTH, ATTN_WIDTH, ATTN_WIDTH, ATTN_HEADS, 2 * CONV_CHANNELS,
             HGRN_WIDTH, HGRN_WIDTH, HGRN_WIDTH, HGRN_WIDTH, D_MODEL, D_MODEL, D_MODEL)
N_IN = sum(IN_SPLITS)
EPS = 1e-6
MASK_VALUE = -1e30
MIN_FORGET = 1e-30

kernel_name = 'hybrid_fox_conformer_hgrn2_block'


def rms_norm(x, g):
    xf = x.astype(jnp.float32)
    y = xf * lax.rsqrt(jnp.mean(xf * xf, axis=-1, keepdims=True) + EPS)
    return (y * g.astype(jnp.float32)).astype(x.dtype)


def layer_norm(x, g, b):
    xf = x.astype(jnp.float32)
    mu = jnp.mean(xf, axis=-1, keepdims=True)
    xc = xf - mu
    y = xc * lax.rsqrt(jnp.mean(xc * xc, axis=-1, keepdims=True) + EPS)
    return (y * g.astype(jnp.float32) + b.astype(jnp.float32)).astype(x.dtype)


def causal_depthwise_conv(x, w):
    k = w.shape[0]
    return lax.conv_general_dilated(
        x, w[:, None, :].astype(x.dtype), window_strides=(1,), padding=[(k - 1, 0)],
        dimension_numbers=('NWC', 'WIO', 'NWC'), feature_group_count=x.shape[-1])


def forgetting_attention(q, k, v, log_f):
    b, s, h, dh = q.shape
    nblk = s // Q_BLOCK
    qf = q.astype(jnp.float32) * (dh ** -0.5)
    kf = k.astype(jnp.float32)
    vf = v.astype(jnp.float32)
    c = jnp.cumsum(log_f.astype(jnp.float32), axis=1).transpose(0, 2, 1)
    q_blocks = qf.reshape(b, nblk, Q_BLOCK, h, dh).transpose(1, 0, 2, 3, 4)
    c_blocks = c.reshape(b, h, nblk, Q_BLOCK).transpose(2, 0, 1, 3)
    k_pos = jnp.arange(s)

    def one_block(args):
        q_blk, c_blk, blk = args
        q_pos = blk * Q_BLOCK + jnp.arange(Q_BLOCK)
        logits = jnp.einsum('bqhd,bkhd->bhqk', q_blk, kf) + (c_blk[..., :, None] - c[:, :, None, :])
        logits = jnp.where(k_pos[None, :] <= q_pos[:, None], logits, MASK_VALUE)
        p = jax.nn.softmax(logits, axis=-1)
        return jnp.einsum('bhqk,bkhd->bqhd', p, vf)

    out = lax.map(one_block, (q_blocks, c_blocks, jnp.arange(nblk)))
    return out.transpose(1, 0, 2, 3, 4).reshape(b, s, h * dh)


def hgrn2_recurrence(q, k, v, log_f):
    b, s, h, dk = q.shape
    dv = v.shape[-1]
    n = s // HGRN_CHUNK

    def to_chunks(t):
        return t.reshape(b, n, HGRN_CHUNK, h, t.shape[-1]).transpose(1, 0, 3, 2, 4)

    causal = jnp.tril(jnp.ones((HGRN_CHUNK, HGRN_CHUNK), dtype=bool))[:, :, None]

    def step(state, inp):
        qc, kc, vc, ac = inp
        cum = jnp.cumsum(ac, axis=2)
        o_inter = jnp.einsum('bhtd,bhde->bhte', qc * jnp.exp(cum), state)
        diff = cum[:, :, :, None, :] - cum[:, :, None, :, :]
        decay = jnp.where(causal, jnp.exp(jnp.where(causal, diff, 0.0)), 0.0)
        scores = jnp.einsum('bhtd,bhsd,bhtsd->bhts', qc, kc, decay)
        o_intra = jnp.einsum('bhts,bhse->bhte', scores, vc)
        cum_last = cum[:, :, -1:, :]
        new_state = (jnp.exp(cum_last[:, :, 0, :])[..., None] * state
                     + jnp.einsum('bhsd,bhse->bhde', kc * jnp.exp(cum_last - cum), vc))
        return new_state, o_inter + o_intra

    s0 = jnp.zeros((b, h, dk, dv), jnp.float32)
    _, o = lax.scan(step, s0, (to_chunks(q), to_chunks(k), to_chunks(v), to_chunks(log_f)))
    return o.transpose(1, 0, 3, 2, 4).reshape(b, s, h, dv)


def conv_ffn(h, w_up, w_dw, w_down):
    u = causal_depthwise_conv(h @ w_up, w_dw)
    gate, up = jnp.split(u, 2, axis=-1)
    return (jax.nn.silu(gate) * up) @ w_down


def setup_inputs(seed: int = 0) -> dict:
    key = jax.random.key(seed)
    ks = jax.random.split(key, 17)

    def normal(k, shape, scale):
        return jax.random.normal(k, shape, jnp.float32) * scale

    return {
        'x': normal(ks[0], (BATCH, SEQ, D_MODEL), 1.0),
        'norm_gains': 1.0 + normal(ks[1], (DEPTH, 4, D_MODEL), 0.02),
        'w_in': normal(ks[2], (DEPTH, D_MODEL, N_IN), D_MODEL ** -0.5),
        'b_fgate': normal(ks[3], (DEPTH, ATTN_HEADS), 0.1),
        'conv_dw': normal(ks[4], (DEPTH, CONV_WIDTH, CONV_CHANNELS), CONV_WIDTH ** -0.5),
        'conv_b': normal(ks[5], (DEPTH, CONV_CHANNELS), 0.02),
        'conv_ln_g': 1.0 + normal(ks[6], (DEPTH, CONV_CHANNELS), 0.02),
        'conv_ln_b': normal(ks[7], (DEPTH, CONV_CHANNELS), 0.02),
        'hgrn_lb_logits': normal(ks[8], (DEPTH, HGRN_WIDTH), 0.5),
        'hgrn_norm_g': 1.0 + normal(ks[9], (DEPTH, HGRN_WIDTH), 0.02),
        'w_attn_out': normal(ks[10], (DEPTH, ATTN_WIDTH, D_MODEL), ATTN_WIDTH ** -0.5),
        'w_conv_out': normal(ks[11], (DEPTH, CONV_CHANNELS, D_MODEL), CONV_CHANNELS ** -0.5),
        'w_hgrn_out': normal(ks[12], (DEPTH, HGRN_WIDTH, D_MODEL), HGRN_WIDTH ** -0.5),
        'w_o': normal(ks[13], (DEPTH, D_MODEL, D_MODEL), D_MODEL ** -0.5),
        'w_ffn_up': normal(ks[14], (DEPTH, D_MODEL, 2 * D_FF), D_MODEL ** -0.5),
        'ffn_dw': normal(ks[15], (DEPTH, FFN_CONV_WIDTH, 2 * D_FF), FFN_CONV_WIDTH ** -0.5),
        'w_ffn_down': normal(ks[16], (DEPTH, D_FF, D_MODEL), D_FF ** -0.5),
    }


def reference(x, norm_gains, w_in, b_fgate, conv_dw, conv_b, conv_ln_g, conv_ln_b,
              hgrn_lb_logits, hgrn_norm_g, w_attn_out, w_conv_out, w_hgrn_out, w_o,
              w_ffn_up, ffn_dw, w_ffn_down):
    b, s, _ = x.shape
    f32 = jnp.float32
    p_lb = jax.nn.softmax(hgrn_lb_logits.astype(f32), axis=0)
    lower_bounds = jnp.cumsum(p_lb, axis=0) - p_lb[0]
    offsets = np.cumsum(IN_SPLITS)[:-1].tolist()

    for l in range(DEPTH):
        h = rms_norm(x, norm_gains[l, 0])
        z = h @ w_in[l]
        (q, k, v, fg, glu, hq, hf, hi, hg, g_a, g_b, g_c) = jnp.split(z, offsets, axis=-1)

        log_f = jax.nn.log_sigmoid(fg.astype(f32) + b_fgate[l].astype(f32))
        att = forgetting_attention(q.reshape(b, s, ATTN_HEADS, ATTN_HEAD_DIM),
                                   k.reshape(b, s, ATTN_HEADS, ATTN_HEAD_DIM),
                                   v.reshape(b, s, ATTN_HEADS, ATTN_HEAD_DIM), log_f)
        y_a = att.astype(x.dtype) @ w_attn_out[l]

        val, gate = jnp.split(glu, 2, axis=-1)
        u = val * jax.nn.sigmoid(gate)
        u = causal_depthwise_conv(u, conv_dw[l]) + conv_b[l]
        u = jax.nn.silu(layer_norm(u, conv_ln_g[l], conv_ln_b[l]))
        y_b = u @ w_conv_out[l]

        lb = lower_bounds[l].reshape(HGRN_HEADS, HGRN_HEAD_DIM)
        zf = hf.astype(f32).reshape(b, s, HGRN_HEADS, HGRN_HEAD_DIM)
        forget = lb + (1.0 - lb) * jax.nn.sigmoid(zf)
        log_forget = jnp.log(jnp.maximum(forget, MIN_FORGET))
        key_in = (1.0 - lb) * jax.nn.sigmoid(-zf)
        query = jax.nn.silu(hq.astype(f32)).reshape(b, s, HGRN_HEADS, HGRN_HEAD_DIM) * (HGRN_HEAD_DIM ** -0.5)
        o = hgrn2_recurrence(query, key_in,
                             hi.astype(f32).reshape(b, s, HGRN_HEADS, HGRN_HEAD_DIM), log_forget)
        o = rms_norm(o, hgrn_norm_g[l].reshape(HGRN_HEADS, HGRN_HEAD_DIM)).reshape(b, s, HGRN_WIDTH)
        o = (o * jax.nn.silu(hg.astype(f32))).astype(x.dtype)
        y_c = o @ w_hgrn_out[l]

        merged = jax.nn.sigmoid(g_a) * y_a + jax.nn.sigmoid(g_b) * y_b + jax.nn.sigmoid(g_c) * y_c
        x = x + rms_norm(merged @ w_o[l], norm_gains[l, 1])

        h = rms_norm(x, norm_gains[l, 2])
        x = x + rms_norm(conv_ffn(h, w_ffn_up[l], ffn_dw[l], w_ffn_down[l]), norm_gains[l, 3])
    return x
```

```python
import os
import numpy as np
import ml_dtypes
from contextlib import ExitStack
import concourse.bass as bass
import concourse.mybir as mybir
from concourse.bass_utils import run_bass_kernel_spmd

F32 = mybir.dt.float32
BF16 = mybir.dt.bfloat16
AF = mybir.ActivationFunctionType
ALU = mybir.AluOpType
NEG = -30000.0
EPS = 1e-6


class Cfg:
    def __init__(s, D=4096, S=16384, HA=16, CC=1024, HH=8, DFF=11008, L=2, NC=8):
        s.D, s.S, s.HA, s.CC, s.HH, s.DFF, s.L, s.NC = D, S, HA, CC, HH, DFF, L, NC
        s.AW = HA * 128
        s.HW = HH * 128
        s.TPC = S // NC
        s.T = 512
        s.NT = s.TPC // s.T
        s.KD = D // 128
        s.KC = CC // 128
        s.KF = DFF // 128
        s.KB = (s.AW + s.CC + s.HW) // 128
        sp = (s.AW, s.AW, s.AW, HA, 2 * CC, s.HW, s.HW, s.HW, s.HW, D, D, D)
        off = np.concatenate([[0], np.cumsum(sp)]).tolist()
        s.NIN = off[-1]
        (s.oq, s.ok, s.ov, s.ofg, s.oglu, s.ohq, s.ohf, s.ohi, s.ohg, s.oga, s.ogb, s.ogc) = off[:-1]
        s.NSUB = s.TPC // 128
        s.NAUG = 9 + s.NSUB
        c = 0
        s.pg = c; c += L * 4 * s.KD
        s.pcw = c; c += L * s.KC * 31
        s.pcb = c; c += L * s.KC
        s.plg = c; c += L * s.KC
        s.plb = c; c += L * s.KC
        s.plog = c; c += L * HH
        s.phg = c; c += L * HH
        s.pfw = c; c += L * 2 * s.KF * 3
        s.pm = c; c += 8
        s.po = c; c += 8
        s.pm64 = c; c += 64
        s.NPP = c


class Res:
    __slots__ = ("name", "lw", "rdc", "rdd", "parent", "children")

    def __init__(self, name, parent=None):
        self.name = name
        self.lw = None
        self.rdc = {}
        self.rdd = []
        self.parent = parent
        self.children = []
        if parent is not None:
            parent.children.append(self)


class Op:
    __slots__ = ("eng", "fn", "kind", "grp", "deps", "signal", "ticket", "idx")

    def __init__(self, eng, fn, kind, grp):
        self.eng, self.fn, self.kind, self.grp = eng, fn, kind, grp
        self.deps = {}
        self.signal = False
        self.ticket = None


class B:
    __slots__ = ("ap", "res")

    def __init__(self, ap, res):
        self.ap, self.res = ap, res

    def __getitem__(self, k):
        return B(self.ap[k], self.res)

    def rr(self, pat, **kw):
        return B(self.ap.rearrange(pat, **kw), self.res)


class Rec:
    ENGS = ("pe", "act", "dve", "pool", "sp")

    def __init__(self):
        self.ops = {e: [] for e in self.ENGS}
        self.n = 0
        self.dq = {e: [] for e in self.ENGS}
        self.KRING = 16
        import os
        self.limit = int(os.environ.get("KOPS", "100000000"))

    def _dep(self, o, p, raw):
        if p is None or p is o:
            return
        o.deps[p] = o.deps.get(p, False) or raw

    def op(self, eng, fn, r=(), w=(), kind="c", grp=None):
        o = Op(eng, fn, kind, grp)
        o.idx = self.n
        self.n += 1
        if self.n == int(os.environ.get("KPRINT", "-1")):
            import traceback
            traceback.print_stack(limit=5)
            print("KPRINT op", eng, kind, [x.name for x in r], [x.name for x in w])
        if self.n > self.limit:
            o.ticket = ("Esp", 0)
            return o
        if kind == "dma":
            q = self.dq[eng]
            o.grp = f"{eng}{len(q) % self.KRING}"
            if len(q) >= self.KRING:
                self._dep(o, q[len(q) - self.KRING], False)
            q.append(o)
        for res in r:
            self._dep(o, res.lw, True)
            if res.parent is not None:
                self._dep(o, res.parent.lw, True)
            for ch in res.children:
                self._dep(o, ch.lw, True)
        for res in w:
            chain = [res] + ([res.parent] if res.parent is not None else []) + list(res.children)
            for q in chain:
                self._dep(o, q.lw, False)
                for p in q.rdc.values():
                    self._dep(o, p, False)
                for p in q.rdd:
                    self._dep(o, p, False)
        for res in r:
            if kind == "c":
                res.rdc[eng] = o
            else:
                res.rdd.append(o)
        for res in w:
            res.lw = o
            res.rdc = {}
            res.rdd = []
            for ch in res.children:
                ch.lw = None
                ch.rdc = {}
                ch.rdd = []
        self.ops[eng].append(o)
        return o

    def finalize(self):
        for e in self.ENGS:
            for o in self.ops[e]:
                keep = {}
                for p, raw in o.deps.items():
                    if p.kind == "c" and o.kind == "c" and p.eng == o.eng:
                        if o.eng == "pe" or not raw:
                            continue
                    keep[p] = raw
                    p.signal = True
                o.deps = keep
        self.groups = {}
        for e in self.ENGS:
            cnt = 0
            for o in self.ops[e]:
                if o.kind == "c":
                    if o.signal:
                        cnt += 1
                        o.ticket = ("E" + e, cnt)
                else:
                    inc = 16 if o.kind == "dma" else 1
                    g = self.groups.get(o.grp, 0) + inc
                    self.groups[o.grp] = g
                    o.ticket = ("G" + o.grp, g)

    def emit(self, nc, es, final_waits=()):
        self.finalize()
        sems = {}

        def sem(key):
            if key not in sems:
                sems[key] = es.enter_context(nc.semaphore(key[:24] + str(len(sems))))
            return sems[key]

        for e in self.ENGS:
            sem("E" + e)
        for g in self.groups:
            sem("G" + g)
        self.nsem = len(sems)

        def run(e, eng):
            seen = {}
            for o in self.ops[e]:
                need = {}
                for p in o.deps:
                    k, v = p.ticket
                    if v > need.get(k, 0):
                        need[k] = v
                for k, v in need.items():
                    if v > seen.get(k, 0):
                        eng.wait_ge(sem(k), v)
                        seen[k] = v
                inst = o.fn(eng)
                if o.kind == "c":
                    if o.signal:
                        inst.then_inc(sem(o.ticket[0]), 1)
                elif o.kind == "dma":
                    inst.then_inc(sem(o.ticket[0]), 16)
                else:
                    inst.then_inc(sem(o.ticket[0]))
            if e == "sp":
                for p in final_waits:
                    k, v = p.ticket
                    eng.wait_ge(sem(k), v)

        with nc.Block() as block:
            block.tensor(lambda eng: run("pe", eng))
            block.scalar(lambda eng: run("act", eng))
            block.vector(lambda eng: run("dve", eng))
            block.gpsimd(lambda eng: run("pool", eng))
            block.sync(lambda eng: run("sp", eng))


class Builder:
    def __init__(self, cfg):
        self.c = cfg
        self.nc = bass.Bass("TRN2", target_bir_lowering=False)
        self.R = Rec()
        self.es = ExitStack()
        self.dres = {}
        self.outs = []
        self.rot = {}

    def dt_(self, name, shape, dtype, kind="Internal"):
        return self.nc.dram_tensor(name, list(shape), dtype, kind=kind)

    def D(self, t, key, idx):
        k = (t.name, key)
        if k not in self.dres:
            self.dres[k] = Res(f"{t.name}:{key}")
        return B(t.ap()[idx], self.dres[k])

    def Dall(self, t):
        return [r for (n, _), r in self.dres.items() if n == t.name]

    def dma(self, out, in_, q="sp", extra_r=(), extra_w=()):
        o_ap, i_ap = out.ap, in_.ap
        return self.R.op(q, lambda e: e.dma_start(out=o_ap, in_=i_ap), r=[in_.res] + list(extra_r),
                         w=[out.res] + list(extra_w), kind="dma")

    def mm(self, out, lhsT, rhs, start, stop):
        o, l, r_ = out.ap, lhsT.ap, rhs.ap
        return self.R.op("pe", lambda e: e.matmul(o, l, r_, start=start, stop=stop), r=[lhsT.res, rhs.res], w=[out.res])

    def tr(self, out, in_, ident):
        o, i, d = out.ap, in_.ap, ident.ap
        return self.R.op("pe", lambda e: e.transpose(o, i, d), r=[in_.res, ident.res], w=[out.res])

    def act(self, out, in_, func, scale=1.0, bias=0.0, eng="act"):
        rs = [in_.res]
        sc = scale
        bi = bias
        if isinstance(scale, B):
            rs.append(scale.res)
            sc = scale.ap
        if isinstance(bias, B):
            rs.append(bias.res)
            bi = bias.ap
        o, i = out.ap, in_.ap
        return self.R.op("act", lambda e: e.activation(out=o, in_=i, func=func, bias=bi, scale=sc), r=rs, w=[out.res])

    def tt(self, out, a, b, op, eng="dve"):
        o, x, y = out.ap, a.ap, b.ap
        return self.R.op(eng, lambda e: e.tensor_tensor(out=o, in0=x, in1=y, op=op), r=[a.res, b.res], w=[out.res])

    def ts(self, out, a, s1, s2=None, op0=ALU.mult, op1=None, eng="dve"):
        rs = [a.res]
        v1, v2 = s1, s2
        if isinstance(s1, B):
            rs.append(s1.res)
            v1 = s1.ap
        if isinstance(s2, B):
            rs.append(s2.res)
            v2 = s2.ap
        o, x = out.ap, a.ap
        if op1 is None:
            return self.R.op(eng, lambda e: e.tensor_scalar(out=o, in0=x, scalar1=v1, scalar2=None, op0=op0), r=rs, w=[out.res])
        return self.R.op(eng, lambda e: e.tensor_scalar(out=o, in0=x, scalar1=v1, scalar2=v2, op0=op0, op1=op1), r=rs, w=[out.res])

    def stt(self, out, a, s, b, op0, op1):
        rs = [a.res, b.res]
        v = s
        if isinstance(s, B):
            rs.append(s.res)
            v = s.ap
        o, x, y = out.ap, a.ap, b.ap
        return self.R.op("dve", lambda e: e.scalar_tensor_tensor(out=o, in0=x, scalar=v, in1=y, op0=op0, op1=op1), r=rs, w=[out.res])

    def cp(self, out, in_, eng="dve"):
        o, i = out.ap, in_.ap
        if eng == "act":
            return self.R.op("act", lambda e: e.copy(out=o, in_=i), r=[in_.res], w=[out.res])
        return self.R.op(eng, lambda e: e.tensor_copy(out=o, in_=i), r=[in_.res], w=[out.res])

    def recip(self, out, in_):
        o, i = out.ap, in_.ap
        return self.R.op("dve", lambda e: e.reciprocal(out=o, in_=i), r=[in_.res], w=[out.res])

    def memset(self, out, val, eng="dve"):
        o = out.ap
        return self.R.op(eng, lambda e: e.memset(o, val), r=[], w=[out.res])

    def scan(self, out, d0, d1, init=0.0):
        o, a, b = out.ap, d0.ap, d1.ap
        return self.R.op("dve", lambda e: e.tensor_tensor_scan(out=o, data0=a, data1=b, initial=init, op0=ALU.mult, op1=ALU.add),
                         r=[d0.res, d1.res], w=[out.res])

    def ag(self, out_t, in_t, r, w, name):
        nranks = self.c.NC
        import os
        if os.environ.get("KNOAG"):
            nr = in_t.ap().shape[0]
            reps = out_t.ap().shape[0] // nr
            o = None
            for i in range(reps):
                o = self.R.op("pool", lambda e, i=i: e.dma_start(out=out_t.ap()[i * nr:(i + 1) * nr], in_=in_t.ap()), r=r, w=w, kind="dma")
            return o
        if not hasattr(self, "ccres"):
            self.ccres = Res("ccchain")
            self.agn = 0
        R_, C_ = list(in_t.ap().shape)
        esz = 4 if in_t.ap().dtype == F32 else 2
        nr = max(1, min(R_, (512 * 1024) // (C_ * esz)))
        key = (nr, C_, esz)
        if not hasattr(self, "agtmp"):
            self.agtmp = {}
        if key not in self.agtmp:
            i = len(self.agtmp)
            self.agtmp[key] = (self.dt_(f"agm{i}", [4 * nr, C_], in_t.ap().dtype),
                               [self.dt_(f"ago{i}_{k}", [8 * nr, C_], in_t.ap().dtype) for k in range(2)])
        mid, outs2 = self.agtmp[key]
        midres = self.D(mid, "all", (slice(None), slice(None))).res
        last = None
        for r0 in range(0, R_, nr):
            n = min(nr, R_ - r0)
            o2 = outs2[self.agn % 2]
            self.agn += 1
            o2res = self.D(o2, "all", (slice(None), slice(None))).res
            i_ap = in_t.ap()[r0:r0 + n, :].opt()
            m_ap = mid.ap()[0:4 * n, :].opt()
            o_ap = o2.ap()[0:8 * n, :].opt()
            self.R.op("pool", lambda e, i_ap=i_ap, m_ap=m_ap: e.collective_compute(
                "AllGather", ALU.bypass, replica_groups=[[0, 1, 2, 3], [4, 5, 6, 7]], ins=[i_ap], outs=[m_ap]),
                r=r, w=[midres, self.ccres], kind="cc", grp="AG1")
            self.R.op("pool", lambda e, m_ap=m_ap, o_ap=o_ap: e.collective_compute(
                "AllGather", ALU.bypass, replica_groups=[[0, 4], [1, 5], [2, 6], [3, 7]], ins=[m_ap], outs=[o_ap]),
                r=[midres], w=[o2res, self.ccres], kind="cc", grp="AG2")
            dst = out_t.ap().rearrange("(j r) c -> j r c", j=8)[:, r0:r0 + n, :]
            src = o2.ap()[0:8 * n, :].rearrange("(j r) c -> j r c", j=8)
            last = self.R.op("pool", lambda e, dst=dst, src=src: e.dma_start(out=dst, in_=src), r=[o2res], w=w, kind="dma")
        return last

    def alloc(self):
        c = self.c
        nc = self.nc
        es = self.es
        self.BIG = 32768
        self.big = []
        for i in range(5):
            t = es.enter_context(nc.sbuf_tensor(f"big{i}", [128, self.BIG // 2], BF16))
            self.big.append((t, Res(f"big{i}")))
        self.stg = []
        for i in range(8):
            t = es.enter_context(nc.sbuf_tensor(f"stg{i}", [128, 1040], BF16))
            self.stg.append((t, Res(f"stg{i}")))
        self.pp_t = es.enter_context(nc.sbuf_tensor("sb_pp", [128, c.NPP], F32))
        self.pp = B(self.pp_t[:], Res("pp"))
        self.cb_t = es.enter_context(nc.sbuf_tensor("sb_cb", [128, 128 * 11], BF16))
        self.cb = B(self.cb_t[:], Res("cb"))
        self.onesf_t = es.enter_context(nc.sbuf_tensor("onesf", [128, 128], F32))
        self.onesf = B(self.onesf_t[:], Res("onesf"))
        self.ones_t = es.enter_context(nc.sbuf_tensor("onesb", [128, c.TPC], BF16))
        self.onesb = B(self.ones_t[:], Res("onesb"))
        self.misc_t = es.enter_context(nc.sbuf_tensor("misc", [128, 1024], F32))
        self.misc = Res("misc")
        self.psum = []
        for i in range(8):
            t = es.enter_context(nc.psum_tensor(f"ps{i}", [128, 512], F32))
            self.psum.append(B(t[:], Res(f"ps{i}")))

    def view(self, slot, off, shape, dtype, name=None, pool=None):
        t, res = (pool or self.big)[slot]
        esz = 4 if dtype == F32 else 2
        n = int(np.prod(shape[1:]))
        ap = t[0:shape[0], off // 2: off // 2 + n * esz // 2]
        if dtype == F32:
            ap = ap.bitcast(F32)
        if len(shape) == 3:
            ap = ap.rearrange("p (a b) -> p a b", b=shape[2])
        if not hasattr(self, "vcache"):
            self.vcache = {}
        if name is None:
            return B(ap, Res(f"{res.name}@{off}", parent=res))
        k = (res.name, name)
        if k not in self.vcache:
            self.vcache[k] = (Res(name, parent=res), off, n * esz)
        assert self.vcache[k][1:] == (off, n * esz), (k, self.vcache[k][1:], off, n * esz)
        return B(ap, self.vcache[k][0])

    def fence(self, slots):
        for s in slots:
            t, res = self.big[s]
            ap = t[0:1, 0:2]
            self.R.op("dve", lambda e, ap=ap: e.memset(ap, 0.0), r=[], w=[res])
            res.children = []
            if hasattr(self, "vcache"):
                for k in [k for k in self.vcache if k[0] == res.name]:
                    del self.vcache[k]

    def stage(self, shape, dtype):
        i = self.rot.get("stg", 0)
        self.rot["stg"] = i + 1
        t, res = self.stg[i % len(self.stg)]
        esz = 4 if dtype == F32 else 2
        n = int(np.prod(shape[1:]))
        ap = t[0:shape[0], 0: n * esz // 2]
        if dtype == F32:
            ap = ap.bitcast(F32)
        return B(ap, res)

    def ps(self, banks):
        key = "ps" + str(banks)
        i = self.rot.get(key, 0)
        self.rot[key] = i + 1
        return self.psum[banks[i % len(banks)]]

    def mi(self, off, n, parts=128):
        return B(self.misc_t[0:parts, off:off + n], self.misc)

    def whole(self, slot, shape, dtype):
        t, res = self.big[slot]
        esz = 4 if dtype == F32 else 2
        n = int(np.prod(shape[1:]))
        ap = t[0:shape[0], 0: n * esz // 2]
        if dtype == F32:
            ap = ap.bitcast(F32)
        if len(shape) == 3:
            ap = ap.rearrange("p (a b) -> p a b", b=shape[2])
        return B(ap, res)

    def msc(self, n, parts=128, name=None):
        if not hasattr(self, "_mc"):
            self._mc = {}
        if name is not None and name in self._mc:
            return self._mc[name]
        o = getattr(self, "_mo", 0)
        self._mo = o + n
        assert self._mo <= 1024, self._mo
        b = B(self.misc_t[0:parts, o:o + n], Res(f"misc{o}"))
        if name is not None:
            self._mc[name] = b
        return b

    def ppc(self, off, n=1):
        return self.pp[:, off:off + n]

    def declare(self):
        c = self.c
        L = c.L
        X = "ExternalInput"
        self.xT = self.dt_("xT", [c.D, c.TPC], F32, X)
        self.wshapes = dict(w_in=(c.D, c.NIN), w_ao=(c.AW, c.D), w_co=(c.CC, c.D), w_ho=(c.HW, c.D),
                            w_o=(c.D, c.D), w_up=(c.D, 2 * c.DFF), w_dn=(c.DFF, c.D))
        self.wext, self.wsh, self.wfull = {}, {}, {}
        self.wdiv = 1 if os.environ.get("KNOAG") else c.NC
        for n, (r, k) in self.wshapes.items():
            self.wext[n] = self.dt_(n + "_s", [L, r // self.wdiv, k], F32, X)
            for l in range(L):
                self.wsh[n, l] = self.dt_(f"{n}_sh{l}", [r // self.wdiv, k], BF16)
                self.wfull[n, l] = self.dt_(f"{n}_f{l}", [r, k], BF16)
        self.pp_d = self.dt_("pp", [128, c.NPP], F32, X)
        self.bfg_d = self.dt_("bfg", [c.HA, L], F32, X)
        self.cb_d = self.dt_("cb", [128, 128 * 11], BF16, X)
        self.augKc = self.dt_("augKc", [c.NAUG, c.S], BF16, X)
        self.augQc = self.dt_("augQc", [c.NAUG, c.TPC], BF16, X)
        self.outT = self.dt_("outT", [c.D, c.TPC], F32, "ExternalOutput")
        I = self.dt_
        self.x1 = I("x1", [c.D, c.TPC], F32)
        self.xmid = I("xmid", [c.D, c.TPC], F32)
        self.mo_d = I("mo_d", [c.D, c.TPC], F32)
        self.hT_d = I("hT_d", [c.D, c.TPC], BF16)
        self.h2_d = I("h2_d", [c.D, c.TPC], BF16)
        self.act_d = I("act_d", [c.DFF, c.TPC], BF16)
        self.brT_d = I("brT_d", [c.KB * 128, c.TPC], BF16)
        self.qT_d = I("qT_d", [c.AW, c.TPC], BF16)
        self.kT_loc = I("kT_loc", [c.AW, c.TPC], BF16)
        self.kT_all = I("kT_all", [c.NC * c.AW, c.TPC], BF16)
        self.v_loc = I("v_loc", [c.TPC, c.AW], BF16)
        self.v_all = I("v_all", [c.S, c.AW], BF16)
        self.cag_loc = I("cag_loc", [c.HA, c.TPC], F32)
        self.cag_all = I("cag_all", [c.NC * c.HA, c.TPC], F32)
        self.augK_d = I("augK_d", [c.HA, c.NAUG, c.S], BF16)
        self.augQ_d = I("augQ_d", [c.HA, c.NAUG, c.TPC], BF16)
        self.u_d = I("u_d", [c.CC, c.TPC], F32)
        self.uh_loc = I("uh_loc", [c.CC, 32], F32)
        self.uh_all = I("uh_all", [c.NC * c.CC, 32], F32)
        self.hq_d = I("hq_d", [c.HW, c.TPC], F32)
        self.lf_d = I("lf_d", [c.HW, c.TPC], F32)
        self.kk_d = I("kk_d", [c.HW, c.TPC], F32)
        self.hgt_d = I("hgt_d", [c.HW, c.TPC], F32)
        self.hv_d = I("hv_d", [c.TPC, c.HW], BF16)
        self.ol_d = I("ol_d", [c.HW, c.TPC], F32)
        self.qg_d = I("qg_d", [c.HW, c.TPC], BF16)
        self.hst_loc = I("hst_loc", [c.HW, 136], F32)
        self.hst_all = I("hst_all", [c.NC * c.HW, 136], F32)
        self.h2h_loc = I("h2h_loc", [c.D, 16], BF16)
        self.h2h_all = I("h2h_all", [c.NC * c.D, 16], BF16)

    def weights_prologue(self, l, names):
        c = self.c
        for n in names:
            r, k = self.wshapes[n]
            rs = r // self.wdiv
            nblk = max(1, min(8, rs // 16))
            step = rs // nblk
            for b in range(nblk):
                sl = slice(b * step, (b + 1) * step if b < nblk - 1 else rs)
                src = B(self.wext[n].ap()[l, sl, :], Res("wext"))
                dst = self.D(self.wsh[n, l], b, (sl, slice(None)))
                self.dma(dst, src, q="pool")
            self.ag(self.wfull[n, l], self.wsh[n, l], r=self.Dall(self.wsh[n, l]),
                    w=[self.D(self.wfull[n, l], "all", (slice(None), slice(None))).res], name=n)

    def wres(self, n, l):
        return self.D(self.wfull[n, l], "all", (slice(None), slice(None))).res

    def setup(self):
        c = self.c
        self.dma(self.pp, B(self.pp_d.ap(), Res("ppd")))
        self.dma(self.cb, B(self.cb_d.ap(), Res("cbd")))
        self.memset(self.onesf, 1.0)
        self.memset(self.onesb, 1.0)
        self.ident = self.cb[:, 0:128]
        self.tri = self.cb[:, 128:256]
        self.ones128 = self.cb[:, 256:384]
        self.negb = self.msc(c.L, c.HA)
        bf = self.msc(c.L, c.HA)
        self.dma(bf, B(self.bfg_d.ap(), Res("bfgd")))
        self.ts(self.negb, bf, -1.0)
        HH = c.HH
        self.lb = self.msc(c.L * HH)
        self.oml = self.msc(c.L * HH)
        self.noml = self.msc(c.L * HH)
        self.memset(self.lb, 0.0)
        d = self.msc(HH)
        self.tt(d, self.ppc(c.plog + HH, HH), self.ppc(c.plog, HH), ALU.subtract)
        self.act(self.lb[:, HH:2 * HH], d, AF.Sigmoid)
        self.ts(self.oml, self.lb, -1.0, 1.0, ALU.mult, ALU.add)
        self.ts(self.noml, self.oml, -1.0)
        for h in range(c.HA):
            self.dma(self.D(self.augK_d, ("c", h), (h, slice(None), slice(None))), B(self.augKc.ap(), Res("akc")), q="act")
            self.dma(self.D(self.augQ_d, ("c", h), (h, slice(None), slice(None))), B(self.augQc.ap(), Res("aqc")), q="act")

    def rms_stats(self, src_fn, nchunk, Dn):
        ssp = self.psum[7]
        for kc in range(nchunk):
            xc = src_fn(kc)
            sq = self.stage([128, 512], F32)
            self.act(sq, xc, AF.Square)
            self.mm(ssp, self.onesf, sq, start=(kc == 0), stop=(kc == nchunk - 1))
        rstd = self.view(4, 16384, [128, 512], F32, name="rstdA")
        self.act(rstd, ssp, AF.Sqrt, scale=1.0 / Dn, bias=self.epsb)
        self.recip(rstd, rstd)
        return rstd

    def ldw(self, slot, n, l, ranges, nk):
        W = self.wfull[n, l]
        wv = W.ap().rearrange("(k p) n -> p k n", p=128)
        ncols = sum(r[1] for r in ranges)
        wt = self.whole(slot, [128, nk, ncols], BF16)
        o = 0
        for (c0, nn) in ranges:
            self.dma(wt[:, :, o:o + nn], B(wv[:, :, c0:c0 + nn], self.wres(n, l)))
            o += nn
        return wt

    def phaseA(self, l):
        c = self.c
        KD = c.KD
        xsrc = self.xT if l == 0 else self.x1
        lnv = self.view(4, 0, [c.HA, c.TPC], F32, name="lnv")
        evi = [0]
        wi = [0]

        def wslot():
            wi[0] += 1
            return 2 + (wi[0] % 2)

        abanks = [0, 1, 2, 3, 4, 5]
        for tt in range(c.NT):
            tsl = slice(tt * 512, (tt + 1) * 512)
            hT = self.whole(tt % 2, [128, KD, 512], BF16)

            def xchunk(kc):
                xc = self.stage([128, 512], F32)
                self.dma(xc, self.D(xsrc, (kc, tt), (slice(kc * 128, (kc + 1) * 128), tsl)))
                return xc
            rstd = self.rms_stats(xchunk, KD, c.D)
            for kc in range(KD):
                xc = xchunk(kc)
                self.stt(hT[:, kc, :], xc, self.ppc(c.pg + (l * 4 + 0) * KD + kc), rstd, ALU.mult, ALU.mult)
            self.dma(self.D(self.hT_d, tt, (slice(None), tsl)).rr("(k p) t -> p k t", p=128), hT, q="act")

            def fm(wt, col, M):
                psb = self.ps(abanks)
                for kc in range(KD):
                    self.mm(psb[0:M, :], wt[:, kc, col:col + M], hT[:, kc, :], start=(kc == 0), stop=(kc == KD - 1))
                return psb

            def tm(wt, ncols, dst_t, c0):
                for t4 in range(4):
                    psb = self.ps(abanks)
                    for kc in range(KD):
                        self.mm(psb[:, 0:ncols], hT[:, kc, t4 * 128:(t4 + 1) * 128], wt[:, kc, 0:ncols],
                                start=(kc == 0), stop=(kc == KD - 1))
                    st = self.stage([128, ncols], BF16)
                    evi[0] += 1
                    self.cp(st, psb[:, 0:ncols], eng=("act" if evi[0] % 2 else "dve"))
                    r0 = tt * 512 + t4 * 128
                    self.dma(self.D(dst_t, (tt, t4, c0), (slice(r0, r0 + 128), slice(c0, c0 + ncols))), st, q="act")

            def store_fm(st, dst_t, row0, M=128):
                self.dma(self.D(dst_t, (row0, tt), (slice(row0, row0 + M), tsl)), st, q="act")

            for (off, dst, isq) in ((c.oq, self.qT_d, True), (c.ok, self.kT_loc, False)):
                for g0 in range(0, c.AW, 512):
                    n = min(512, c.AW - g0)
                    wt = self.ldw(wslot(), "w_in", l, [(off + g0, n)], KD)
                    for j in range(n // 128):
                        psb = fm(wt, j * 128, 128)
                        st = self.stage([128, 512], BF16)
                        if isq:
                            self.act(st, psb, AF.Copy, scale=128.0 ** -0.5)
                        else:
                            self.cp(st, psb)
                        store_fm(st, dst, g0 + j * 128)
            for g0 in range(0, c.AW, 512):
                n = min(512, c.AW - g0)
                wt = self.ldw(wslot(), "w_in", l, [(c.ov + g0, n)], KD)
                tm(wt, n, self.v_loc, g0)
            wt = self.ldw(wslot(), "w_in", l, [(c.ofg, c.HA)], KD)
            psb = fm(wt, 0, c.HA)
            e1 = self.stage([c.HA, 512], F32)
            self.act(e1, psb[0:c.HA, :], AF.Exp, scale=-1.0, bias=self.negb[:, l:l + 1])
            self.act(lnv[:, tsl], e1, AF.Ln, bias=self.oneb[0:c.HA, :])
            gcols = min(256, c.CC)
            for g0 in range(0, c.CC, gcols):
                wt = self.ldw(wslot(), "w_in", l, [(c.oglu + g0, gcols), (c.oglu + c.CC + g0, gcols)], KD)
                for j in range(gcols // 128):
                    pv = fm(wt, j * 128, 128)
                    pg = fm(wt, gcols + j * 128, 128)
                    sg = self.stage([128, 512], F32)
                    self.act(sg, pg, AF.Sigmoid)
                    u = self.stage([128, 512], F32)
                    self.tt(u, pv, sg, ALU.mult)
                    store_fm(u, self.u_d, g0 + j * 128)
            for g0 in range(0, c.HW, 512):
                n = min(512, c.HW - g0)
                wt = self.ldw(wslot(), "w_in", l, [(c.ohq + g0, n)], KD)
                for j in range(n // 128):
                    psb = fm(wt, j * 128, 128)
                    st = self.stage([128, 512], F32)
                    self.act(st, psb, AF.Silu)
                    store_fm(st, self.hq_d, g0 + j * 128)
            for g0 in range(0, c.HW, 512):
                n = min(512, c.HW - g0)
                wt = self.ldw(wslot(), "w_in", l, [(c.ohf + g0, n)], KD)
                for j in range(n // 128):
                    hh = (g0 + j * 128) // 128
                    psb = fm(wt, j * 128, 128)
                    sg = self.stage([128, 512], F32)
                    self.act(sg, psb, AF.Sigmoid)
                    lf = self.stage([128, 512], F32)
                    ci = l * c.HH + hh
                    self.act(lf, sg, AF.Ln, scale=self.oml[:, ci:ci + 1], bias=self.lb[:, ci:ci + 1])
                    store_fm(lf, self.lf_d, g0 + j * 128)
                    kk = self.stage([128, 512], F32)
                    self.ts(kk, sg, self.noml[:, ci:ci + 1], self.oml[:, ci:ci + 1], ALU.mult, ALU.add)
                    store_fm(kk, self.kk_d, g0 + j * 128)
            for g0 in range(0, c.HW, 512):
                n = min(512, c.HW - g0)
                wt = self.ldw(wslot(), "w_in", l, [(c.ohi + g0, n)], KD)
                tm(wt, n, self.hv_d, g0)
            for g0 in range(0, c.HW, 512):
                n = min(512, c.HW - g0)
                wt = self.ldw(wslot(), "w_in", l, [(c.ohg + g0, n)], KD)
                for j in range(n // 128):
                    psb = fm(wt, j * 128, 128)
                    st = self.stage([128, 512], F32)
                    self.act(st, psb, AF.Silu)
                    store_fm(st, self.hgt_d, g0 + j * 128)
        self.dma(self.D(self.uh_loc, 0, (slice(None), slice(None))),
                 B(self.u_d.ap()[:, c.TPC - 32:c.TPC], Res("tmp")), q="act", extra_r=self.Dall(self.u_d))
        cpl = self.view(4, 8192, [c.HA, c.TPC], F32, name="cpl")
        self.scan(cpl, self.onesb[0:c.HA, :], lnv)
        self.dma(self.D(self.cag_loc, 0, (slice(None), slice(None))), cpl, q="act")
        self.cpl = cpl
        self.ag(self.kT_all, self.kT_loc, r=self.Dall(self.kT_loc), w=[self.D(self.kT_all, "all", (slice(None), slice(None))).res], name="k")
        self.ag(self.v_all, self.v_loc, r=self.Dall(self.v_loc), w=[self.D(self.v_all, "all", (slice(None), slice(None))).res], name="v")
        self.ag(self.cag_all, self.cag_loc, r=self.Dall(self.cag_loc), w=[self.D(self.cag_all, "all", (slice(None), slice(None))).res], name="c")
        self.ag(self.uh_all, self.uh_loc, r=self.Dall(self.uh_loc), w=[self.D(self.uh_all, "all", (slice(None), slice(None))).res], name="uh")

    def allres(self, t):
        return self.D(t, "all", (slice(None), slice(None))).res

    def pieces(self, cur, dst_fn):
        c = self.c
        for h0 in range(0, c.TPC, 1024):
            hs = slice(h0, h0 + 1024)
            for i in range(4):
                p = self.stage([c.HA, 1024], BF16)
                self.cp(p, cur[:, hs])
                dst_fn(i, p, hs)
                if i < 3:
                    self.tt(cur[:, hs], cur[:, hs], p, ALU.subtract)

    def phaseCsum(self, l):
        c = self.c
        HA = c.HA
        self.fence([0, 1])
        tot = self.msc(8, HA, "tot")
        self.dma(tot.rr("h (j o) -> h j o", o=1),
                 B(self.cag_all.ap().rearrange("(j h) t -> h j t", h=HA)[:, :, c.TPC - 1:c.TPC], self.allres(self.cag_all)))
        incl = self.msc(8, HA, "incl")
        self.scan(incl, self.onesb[0:HA, 0:8], tot)
        offs = self.msc(8, HA, "offs")
        self.tt(offs, incl, tot, ALU.subtract)
        tm_ = self.msc(8, HA, "tm_")
        self.tt(tm_, tot, self.pp[0:HA, c.pm:c.pm + 8], ALU.mult)
        inclm = self.msc(8, HA, "inclm")
        self.scan(inclm, self.onesb[0:HA, 0:8], tm_)
        for j in range(c.NC):
            cs = self.view(j % 2, 0, [HA, c.TPC], F32, name=f"cseg{j%2}")
            self.dma(cs, B(self.cag_all.ap()[j * HA:(j + 1) * HA, :], self.allres(self.cag_all)))
            self.ts(cs, cs, offs[:, j:j + 1], None, ALU.add)
            self.pieces(cs, lambda i, p, hs, j=j: self.dma(
                self.D(self.augK_d, ("p", i, j, hs.start), (slice(None), 4 + i, slice(j * c.TPC + hs.start, j * c.TPC + hs.stop))), p, q="act"))
        cq = self.view(0, 8192, [HA, c.TPC], F32, name="cq")
        self.ts(cq, self.cpl, inclm[:, 7:8], None, ALU.add)
        self.pieces(cq, lambda i, p, hs: self.dma(self.D(self.augQ_d, ("p", i, hs.start), (slice(None), i, hs)), p, q="act"))

    def phaseAttn(self, l):
        c = self.c
        self.fence([0, 1, 2, 3])
        NS = c.NSUB
        qTs = [self.view(0, i * 4096, [128, c.TPC], BF16, name=f"qT{i}") for i in range(2)]
        aQs = [self.view(0, 8192 + i * 4096, [c.NAUG, c.TPC], BF16, name=f"aQ{i}") for i in range(2)]
        PTs = [self.view(0, 16384 + i * 1024, [128, 512], BF16, name=f"PT{i}") for i in range(4)]
        kss = [self.view(1, i * 4096, [128, c.TPC], BF16, name=f"ks{i}") for i in range(4)]
        aKs = [self.view(1, 16384 + i * 4096, [c.NAUG, c.TPC], BF16, name=f"aK{i}") for i in range(4)]
        vss = [self.view(2, i * 4096, [128, NS, 128], BF16, name=f"vs{i}") for i in range(4)]
        kall, vall = self.allres(self.kT_all), self.allres(self.v_all)
        si = 0
        pti = 0
        for h in range(c.HA):
            qT, aQ = qTs[h % 2], aQs[h % 2]
            self.dma(qT, B(self.qT_d.ap()[h * 128:(h + 1) * 128, :], Res("t")), extra_r=self.Dall(self.qT_d))
            self.dma(aQ, B(self.augQ_d.ap()[h], Res("t")), extra_r=self.Dall(self.augQ_d))
            for qb in range(c.NT):
                qsl = slice(qb * 512, (qb + 1) * 512)
                OT = self.psum[3 + (h * c.NT + qb) % 2]
                DN = self.psum[5 + (h * c.NT + qb) % 2]
                for j in range(c.NC):
                    ks, aK, vs = kss[si % 4], aKs[si % 4], vss[si % 4]
                    si += 1
                    r0 = j * c.AW + h * 128
                    self.dma(ks, B(self.kT_all.ap()[r0:r0 + 128, :], kall))
                    self.dma(aK, B(self.augK_d.ap()[h, :, j * c.TPC:(j + 1) * c.TPC], Res("t")), extra_r=self.Dall(self.augK_d))
                    self.dma(vs, B(self.v_all.ap()[j * c.TPC:(j + 1) * c.TPC, h * 128:(h + 1) * 128].rearrange("(b p) d -> p b d", p=128), vall))
                    for kb in range(NS):
                        S_ = self.ps([0, 1, 2])
                        ksl = slice(kb * 128, (kb + 1) * 128)
                        diag = (kb // 4 == qb)
                        self.mm(S_, ks[:, ksl], qT[:, qsl], start=True, stop=False)
                        self.mm(S_, aK[:, ksl], aQ[:, qsl], start=False, stop=not diag)
                        if diag:
                            sub = kb - 4 * qb
                            self.mm(S_[:, sub * 128:(sub + 1) * 128], self.cb[:, (3 + j) * 128:(4 + j) * 128], self.tri,
                                    start=False, stop=True)
                        PT = PTs[pti % 4]
                        pti += 1
                        self.act(PT, S_, AF.Exp)
                        first = (j == 0 and kb == 0)
                        last = (j == c.NC - 1 and kb == NS - 1)
                        self.mm(OT, vs[:, kb, :], PT, start=first, stop=last)
                        self.mm(DN, self.ones128, PT, start=first, stop=last)
                rd = self.stage([128, 512], F32)
                self.recip(rd, DN)
                at = self.stage([128, 512], BF16)
                self.tt(at, OT, rd, ALU.mult)
                self.dma(self.D(self.brT_d, ("a", h, qb), (slice(h * 128, (h + 1) * 128), qsl)), at, q="act")

    def phaseConv(self, l):
        c = self.c
        KC = c.KC
        self.fence([0, 1, 2, 3])
        uh = self.view(0, 0, [128, c.NC * KC, 32], F32, name="uh")
        self.dma(uh, B(self.uh_all.ap().rearrange("(j k p) t -> p (j k) t", p=128, k=KC), self.allres(self.uh_all)))
        halo = self.view(0, 16384, [128, KC, 32], F32, name="halo")
        for j in range(c.NC):
            src = uh[:, j * KC:(j + 1) * KC, :]
            if j == 0:
                self.ts(halo, src, self.ppc(c.po + j), None, ALU.mult)
            else:
                self.stt(halo, src, self.ppc(c.po + j), halo, ALU.mult, ALU.add)
        for tt in range(c.NT):
            tsl = slice(tt * 512, (tt + 1) * 512)
            y = self.whole(1, [128, KC, 512], F32)
            for cj in range(KC):
                ub = self.view(2, (cj % 2) * 4096, [128, 544], F32, name=f"ub{cj%2}")
                rows = slice(cj * 128, (cj + 1) * 128)
                if tt == 0:
                    self.cp(ub[:, 0:32], halo[:, cj, :], eng="act")
                    self.dma(ub[:, 32:544], B(self.u_d.ap()[rows, tsl], Res("t")), extra_r=self.Dall(self.u_d))
                else:
                    self.dma(ub, B(self.u_d.ap()[rows, tt * 512 - 32:(tt + 1) * 512], Res("t")), extra_r=self.Dall(self.u_d))
                wb = c.pcw + (l * KC + cj) * 31
                acc = y[:, cj, :]
                self.ts(acc, ub[:, 2:2 + 512], self.ppc(wb), self.ppc(c.pcb + l * KC + cj), ALU.mult, ALU.add)
                for k in range(1, 31):
                    self.stt(acc, ub[:, 2 + k:2 + k + 512], self.ppc(wb + k), acc, ALU.mult, ALU.add)
            s1, s2 = self.psum[0], self.psum[1]
            for cj in range(KC):
                self.mm(s1, self.onesf, y[:, cj, :], start=(cj == 0), stop=(cj == KC - 1))
            for cj in range(KC):
                sq = self.stage([128, 512], F32)
                self.act(sq, y[:, cj, :], AF.Square)
                self.mm(s2, self.onesf, sq, start=(cj == 0), stop=(cj == KC - 1))
            mean = self.view(3, 0, [128, 512], F32, name="cmean")
            self.ts(mean, s1, 1.0 / c.CC)
            msq = self.stage([128, 512], F32)
            self.tt(msq, mean, mean, ALU.mult)
            var = self.stage([128, 512], F32)
            self.stt(var, s2, 1.0 / c.CC, msq, ALU.mult, ALU.subtract)
            rstd = self.view(3, 2048, [128, 512], F32, name="crstd")
            self.act(rstd, var, AF.Sqrt, bias=self.epsb)
            self.recip(rstd, rstd)
            for cj in range(KC):
                t1 = self.stage([128, 512], F32)
                self.tt(t1, y[:, cj, :], mean, ALU.subtract)
                self.tt(t1, t1, rstd, ALU.mult)
                ob = self.stage([128, 512], BF16)
                self.act(ob, t1, AF.Silu, scale=self.ppc(c.plg + l * KC + cj), bias=self.ppc(c.plb + l * KC + cj))
                r0 = c.AW + cj * 128
                self.dma(self.D(self.brT_d, ("b", cj, tt), (slice(r0, r0 + 128), tsl)), ob, q="act")

    def phaseHgrn1(self, l):
        c = self.c
        TPC = c.TPC
        CS = 32
        NCH = TPC // CS
        PB = 512 // CS
        self.fence([0, 1, 2, 3, 4])
        msk = self.pp[0:CS, c.pm64:c.pm64 + CS]
        for hh in range(c.HH):
            rows = slice(hh * 128, (hh + 1) * 128)
            sl = hh % 2
            V = lambda i: self.view(sl, i * 8192, [128, TPC], F32, name=f"hg{sl}_{i}")
            qs, lf, kk, cum = V(0), V(1), V(2), V(3)
            cumg = self.view(2 + sl, 0, [128, TPC], F32, name=f"hw{sl}_0")
            ec = self.view(2 + sl, 8192, [128, TPC], F32, name=f"hw{sl}_1")
            vt = self.view(2 + sl, 16384, [CS, NCH, 128], BF16, name=f"vt{sl}")
            qt = self.view(4, sl * 16384, [128, TPC], BF16, name=f"qt{sl}")
            kt = self.view(4, sl * 16384 + 4096, [128, TPC], BF16, name=f"kt{sl}")
            qg = self.view(4, sl * 16384 + 8192, [128, TPC], BF16, name=f"qg{sl}")
            qi = self.view(4, sl * 16384 + 12288, [128, TPC], BF16, name=f"qi{sl}")
            self.dma(qs, B(self.hq_d.ap()[rows, :], Res("t")), extra_r=self.Dall(self.hq_d))
            self.dma(lf, B(self.lf_d.ap()[rows, :], Res("t")), extra_r=self.Dall(self.lf_d))
            self.dma(kk, B(self.kk_d.ap()[rows, :], Res("t")), extra_r=self.Dall(self.kk_d))
            self.dma(vt, B(self.hv_d.ap()[:, rows].rearrange("(n s) d -> s n d", s=CS), Res("t")), extra_r=self.Dall(self.hv_d))
            for ch in range(NCH):
                cs = slice(ch * CS, (ch + 1) * CS)
                self.scan(cum[:, cs], self.onesb[:, 0:CS], lf[:, cs])
            self.scan(cumg, self.onesb[:, 0:TPC], lf)
            ncum = lf
            self.ts(ncum, cum, -1.0)
            for ch in range(NCH):
                cs = slice(ch * CS, (ch + 1) * CS)
                mid = ch * CS + CS // 2 - 1
                self.act(ec[:, cs], cum[:, cs], AF.Exp, bias=ncum[:, mid:mid + 1])
            self.stt(qt, qs, 128.0 ** -0.5, ec, ALU.mult, ALU.mult)
            for ch in range(NCH):
                cs = slice(ch * CS, (ch + 1) * CS)
                mid = ch * CS + CS // 2 - 1
                self.act(ec[:, cs], cum[:, cs], AF.Exp, scale=-1.0, bias=cum[:, mid:mid + 1])
            self.tt(kt, kk, ec, ALU.mult)
            self.act(ec, cum, AF.Exp)
            self.stt(qi, qs, 128.0 ** -0.5, ec, ALU.mult, ALU.mult)
            self.act(ec, cumg, AF.Exp)
            self.stt(qg, qs, 128.0 ** -0.5, ec, ALU.mult, ALU.mult)
            self.dma(self.D(self.qg_d, hh, (rows, slice(None))), qg, q="act")
            oloc = qs
            Sf = self.msc(128, 128, "Sf")
            Sbt = None
            dec = self.msc(1, 128, "dec")
            for ch in range(NCH):
                cs = slice(ch * CS, (ch + 1) * CS)
                last = ch * CS + CS - 1
                sc = self.ps([0, 1])
                self.mm(sc[0:CS, 0:CS], kt[:, cs], qt[:, cs], start=True, stop=True)
                A = self.stage([CS, CS], BF16)
                self.tt(A, sc[0:CS, 0:CS], msk, ALU.mult)
                ob = self.psum[4 + (ch // PB) % 2]
                oc = ob[:, (ch % PB) * CS:(ch % PB + 1) * CS]
                self.mm(oc, vt[:, ch, :], A, start=True, stop=(ch == 0))
                if ch > 0:
                    self.mm(oc, Sbt, qi[:, cs], start=False, stop=True)
                if ch % PB == PB - 1:
                    self.cp(oloc[:, (ch - PB + 1) * CS:(ch + 1) * CS], ob, eng="act")
                kf = self.stage([128, CS], F32)
                self.act(kf, cum[:, cs], AF.Exp, scale=-1.0, bias=cum[:, last:last + 1])
                kh = self.stage([128, CS], BF16)
                self.tt(kh, kf, kk[:, cs], ALU.mult)
                tp = self.ps([2, 3])
                tpb = B(tp.ap[0:CS, 0:64].bitcast(BF16), tp.res)
                self.tr(tpb, kh, self.ident)
                kht = self.stage([CS, 128], BF16)
                self.cp(kht, tpb, eng="act")
                sn = self.ps([6, 7])
                self.mm(sn[:, 0:128], kht, vt[:, ch, :], start=True, stop=True)
                self.act(dec, cum[:, last:last + 1], AF.Exp)
                if ch == 0:
                    self.cp(Sf, sn[:, 0:128])
                else:
                    self.stt(Sf, Sf, dec, sn[:, 0:128], ALU.mult, ALU.add)
                Sbt = self.stage([128, 128], BF16)
                self.cp(Sbt, Sf)
            self.dma(self.D(self.ol_d, hh, (rows, slice(None))), oloc, q="act")
            self.dma(self.D(self.hst_loc, ("s", hh), (rows, slice(0, 128))), Sf, q="act")
            self.dma(self.D(self.hst_loc, ("d", hh), (rows, slice(128, 129))), cumg[:, TPC - 1:TPC], q="act")
        self.ag(self.hst_all, self.hst_loc, r=self.Dall(self.hst_loc), w=[self.allres(self.hst_all)], name="hst")

    def phaseHgrn2(self, l):
        c = self.c
        TPC = c.TPC
        self.fence([0, 1, 2, 3, 4])
        NCr = c.NC
        for hh in range(c.HH):
            rows = slice(hh * 128, (hh + 1) * 128)
            sl = hh % 2
            st = self.view(sl, 0, [128, NCr, 136], F32, name=f"hst{sl}")
            self.dma(st, B(self.hst_all.ap().rearrange("(j r) n -> r j n", r=c.HW)[rows, :, :], self.allres(self.hst_all)))
            md = self.msc(8, 128, "md")
            self.tt(md.rr("p (j o) -> p j o", o=1), st[:, :, 128:129], self.pp[:, c.pm:c.pm + 8].rr("p (j o) -> p j o", o=1), ALU.mult)
            sfx = self.msc(8, 128, "sfx")
            self.memset(sfx, 0.0)
            for j in range(NCr - 2, -1, -1):
                self.tt(sfx[:, j:j + 1], sfx[:, j + 1:j + 2], md[:, j + 1:j + 2], ALU.add)
            w = self.msc(8, 128, "w8")
            self.act(w, sfx, AF.Exp)
            self.tt(w, w, self.pp[:, c.pm:c.pm + 8], ALU.mult)
            S0 = self.msc(128, 128, "S0")
            for j in range(NCr):
                if j == 0:
                    self.ts(S0, st[:, j, 0:128], w[:, j:j + 1], None, ALU.mult)
                else:
                    self.stt(S0, st[:, j, 0:128], w[:, j:j + 1], S0, ALU.mult, ALU.add)
            S0b = self.view(sl, 8192, [128, 128], BF16, name=f"S0b{sl}")
            self.cp(S0b, S0)
            qg = self.view(2 + sl, 0, [128, TPC], BF16, name=f"qg2{sl}")
            ol = self.view(2 + sl, 8192, [128, TPC], F32, name=f"ol{sl}")
            hg = self.view(2 + sl, 16384, [128, TPC], F32, name=f"hgt{sl}")
            self.dma(qg, B(self.qg_d.ap()[rows, :], Res("t")), extra_r=self.Dall(self.qg_d))
            self.dma(ol, B(self.ol_d.ap()[rows, :], Res("t")), extra_r=self.Dall(self.ol_d))
            self.dma(hg, B(self.hgt_d.ap()[rows, :], Res("t")), extra_r=self.Dall(self.hgt_d))
            for tt in range(c.NT):
                tsl = slice(tt * 512, (tt + 1) * 512)
                cp_ = self.ps([0, 1])
                self.mm(cp_, S0b, qg[:, tsl], start=True, stop=True)
                o = self.stage([128, 512], F32)
                self.tt(o, cp_, ol[:, tsl], ALU.add)
                sq = self.stage([128, 512], F32)
                self.act(sq, o, AF.Square)
                ss = self.ps([2, 3])
                self.mm(ss, self.onesf, sq, start=True, stop=True)
                rstd = self.stage([128, 512], F32)
                self.act(rstd, ss, AF.Sqrt, scale=1.0 / 128, bias=self.epsb)
                self.recip(rstd, rstd)
                self.tt(o, o, rstd, ALU.mult)
                ob = self.stage([128, 512], BF16)
                self.stt(ob, o, self.ppc(c.phg + l * c.HH + hh), hg[:, tsl], ALU.mult, ALU.mult)
                r0 = c.AW + c.CC + hh * 128
                self.dma(self.D(self.brT_d, ("c", hh, tt), (slice(r0, r0 + 128), tsl)), ob, q="act")

    def phaseC(self, l):
        c = self.c
        KD = c.KD
        self.fence([0, 1, 2, 3, 4])
        xsrc = self.xT if l == 0 else self.x1
        wi = 0
        ka, kc_, kh_ = c.AW // 128, c.CC // 128, c.HW // 128
        for tt in range(c.NT):
            tsl = slice(tt * 512, (tt + 1) * 512)
            hT = self.whole(0, [128, KD, 512], BF16)
            br = self.whole(1, [128, c.KB, 512], BF16)
            mg = self.whole(2, [128, KD, 512], BF16)
            self.dma(hT, B(self.hT_d.ap()[:, tsl].rearrange("(k p) t -> p k t", p=128), Res("t")), extra_r=self.Dall(self.hT_d))
            self.dma(br, B(self.brT_d.ap()[:, tsl].rearrange("(k p) t -> p k t", p=128), Res("t")), extra_r=self.Dall(self.brT_d))
            for fc in range(KD):
                fs = slice(fc * 128, (fc + 1) * 128)
                slot = 3 + wi % 2
                wi += 1
                o = 0
                wts = {}
                for (nm, nk, c0) in (("w_ao", ka, fc * 128), ("w_co", kc_, fc * 128), ("w_ho", kh_, fc * 128),
                                     ("w_in", KD, c.oga + fc * 128), ("w_in", KD, c.ogb + fc * 128), ("w_in", KD, c.ogc + fc * 128)):
                    v = self.view(slot, o, [128, nk, 128], BF16, name=f"wc{slot}_{o}")
                    o += nk * 256
                    wv = self.wfull[nm, l].ap().rearrange("(k p) n -> p k n", p=128)
                    self.dma(v, B(wv[:, :, c0:c0 + 128], self.wres(nm, l)))
                    wts[len(wts)] = v
                accs = []
                boff = 0
                for i, nk in enumerate((ka, kc_, kh_)):
                    psb = self.psum[i]
                    for k in range(nk):
                        self.mm(psb, wts[i][:, k, :], br[:, boff + k, :], start=(k == 0), stop=(k == nk - 1))
                    boff += nk
                    accs.append(psb)
                gs = []
                for i in range(3):
                    psb = self.psum[3 + i]
                    for k in range(KD):
                        self.mm(psb, wts[3 + i][:, k, :], hT[:, k, :], start=(k == 0), stop=(k == KD - 1))
                    sg = self.stage([128, 512], F32)
                    self.act(sg, psb, AF.Sigmoid)
                    gs.append(sg)
                m = self.stage([128, 512], F32)
                self.tt(m, accs[0], gs[0], ALU.mult)
                t2 = self.stage([128, 512], F32)
                self.tt(t2, accs[1], gs[1], ALU.mult)
                self.tt(m, m, t2, ALU.add)
                self.tt(t2, accs[2], gs[2], ALU.mult)
                self.tt(mg[:, fc, :], m, t2, ALU.add)
            self.fence([3, 4])
            ssp = self.psum[7]
            for fc in range(KD):
                slot = 3 + wi % 2
                wi += 1
                wt = self.ldw(slot, "w_o", l, [(fc * 128, 128)], KD)
                psb = self.ps([0, 1, 2, 3])
                for k in range(KD):
                    self.mm(psb, wt[:, k, :], mg[:, k, :], start=(k == 0), stop=(k == KD - 1))
                mo = self.stage([128, 512], F32)
                self.cp(mo, psb)
                self.dma(self.D(self.mo_d, (fc, tt), (slice(fc * 128, (fc + 1) * 128), tsl)), mo, q="act")
                sq = self.stage([128, 512], F32)
                self.act(sq, mo, AF.Square)
                self.mm(ssp, self.onesf, sq, start=(fc == 0), stop=(fc == KD - 1))
            rstd = self.stage([128, 512], F32)
            self.act(rstd, ssp, AF.Sqrt, scale=1.0 / c.D, bias=self.epsb)
            self.recip(rstd, rstd)
            rs1 = self.view(3, 0, [128, 512], F32, name="rs1")
            self.cp(rs1, rstd)
            ss2 = self.psum[6]
            for fc in range(KD):
                rows = slice(fc * 128, (fc + 1) * 128)
                mo = self.stage([128, 512], F32)
                self.dma(mo, self.D(self.mo_d, (fc, tt), (rows, tsl)))
                xc = self.stage([128, 512], F32)
                self.dma(xc, self.D(xsrc, (fc, tt), (rows, tsl)))
                self.stt(mo, mo, self.ppc(c.pg + (l * 4 + 1) * KD + fc), rs1, ALU.mult, ALU.mult)
                self.tt(xc, xc, mo, ALU.add)
                self.dma(self.D(self.xmid, (fc, tt), (rows, tsl)), xc, q="act")
                sq = self.stage([128, 512], F32)
                self.act(sq, xc, AF.Square)
                self.mm(ss2, self.onesf, sq, start=(fc == 0), stop=(fc == KD - 1))
            rstd2 = self.view(3, 2048, [128, 512], F32, name="rs2")
            self.act(rstd2, ss2, AF.Sqrt, scale=1.0 / c.D, bias=self.epsb)
            self.recip(rstd2, rstd2)
            for fc in range(KD):
                rows = slice(fc * 128, (fc + 1) * 128)
                xc = self.stage([128, 512], F32)
                self.dma(xc, self.D(self.xmid, (fc, tt), (rows, tsl)))
                h2 = self.stage([128, 512], BF16)
                self.stt(h2, xc, self.ppc(c.pg + (l * 4 + 2) * KD + fc), rstd2, ALU.mult, ALU.mult)
                self.dma(self.D(self.h2_d, (fc, tt), (rows, tsl)), h2, q="act")
                if tt == c.NT - 1:
                    self.dma(self.D(self.h2h_loc, fc, (rows, slice(None))), h2[:, 512 - 16:512], q="act")
            self.fence([3])
        self.ag(self.h2h_all, self.h2h_loc, r=self.Dall(self.h2h_loc), w=[self.allres(self.h2h_all)], name="h2h")

    def phaseD1(self, l):
        c = self.c
        KD, KF = c.KD, c.KF
        self.fence([0, 1, 2, 3, 4])
        hh_ = self.view(2, 0, [128, c.NC * KD, 16], BF16, name="h2hall")
        self.dma(hh_, B(self.h2h_all.ap().rearrange("(j k p) t -> p (j k) t", p=128, k=KD), self.allres(self.h2h_all)))
        hf = self.view(2, 16384, [128, KD, 16], F32, name="h2hf")
        for j in range(c.NC):
            src = hh_[:, j * KD:(j + 1) * KD, :]
            if j == 0:
                self.ts(hf, src, self.ppc(c.po + j), None, ALU.mult)
            else:
                self.stt(hf, src, self.ppc(c.po + j), hf, ALU.mult, ALU.add)
        hb = self.view(2, 16384 + KD * 64, [128, KD, 16], BF16, name="h2hb")
        self.cp(hb, hf)
        carry = self.view(2, 16384 + KD * 96, [128, 2 * KF, 2], F32, name="carry")
        wi = 0
        for tt in range(c.NT):
            tsl = slice(tt * 512, (tt + 1) * 512)
            h2 = self.whole(tt % 2, [128, KD, 512], BF16)
            self.dma(h2, B(self.h2_d.ap()[:, tsl].rearrange("(k p) t -> p k t", p=128), Res("t")), extra_r=self.Dall(self.h2_d))
            for j in range(KF):
                slot = 3 + wi % 2
                wi += 1
                wt = self.ldw(slot, "w_up", l, [(j * 128, 128), (c.DFF + j * 128, 128)], KD)
                outs = []
                for half in range(2):
                    psb = self.ps([0, 1, 2, 3])
                    for k in range(KD):
                        self.mm(psb, wt[:, k, half * 128:(half + 1) * 128], h2[:, k, :], start=(k == 0), stop=(k == KD - 1))
                    ext = self.stage([128, 514], F32)
                    ci = half * KF + j
                    if tt == 0:
                        ph = self.ps([4, 5])
                        for k in range(KD):
                            self.mm(ph[:, 0:16], wt[:, k, half * 128:(half + 1) * 128], hb[:, k, :], start=(k == 0), stop=(k == KD - 1))
                        self.cp(ext[:, 0:2], ph[:, 14:16])
                    else:
                        self.cp(ext[:, 0:2], carry[:, ci, :])
                    self.cp(ext[:, 2:514], psb, eng="act")
                    if tt < c.NT - 1:
                        self.cp(carry[:, ci, :], ext[:, 512:514])
                    wb = c.pfw + (l * 2 * KF + ci) * 3
                    a = self.stage([128, 512], F32)
                    self.ts(a, ext[:, 2:514], self.ppc(wb + 2), None, ALU.mult)
                    self.stt(a, ext[:, 1:513], self.ppc(wb + 1), a, ALU.mult, ALU.add)
                    self.stt(a, ext[:, 0:512], self.ppc(wb + 0), a, ALU.mult, ALU.add)
                    outs.append(a)
                sg = self.stage([128, 512], F32)
                self.act(sg, outs[0], AF.Silu)
                ab = self.stage([128, 512], BF16)
                self.tt(ab, sg, outs[1], ALU.mult)
                self.dma(self.D(self.act_d, (j, tt), (slice(j * 128, (j + 1) * 128), tsl)), ab, q="act")

    def phaseD2(self, l):
        c = self.c
        KD, KF = c.KD, c.KF
        self.fence([0, 1, 2, 3, 4])
        xdst = self.x1 if l == 0 else self.outT
        G = min(6, KD)
        KG = 2 if KF % 2 == 0 else 1
        wi = 0
        for tt in range(c.NT):
            tsl = slice(tt * 512, (tt + 1) * 512)
            ssp = self.psum[7]
            for g0 in range(0, KD, G):
                ng = min(G, KD - g0)
                accs = [self.psum[i] for i in range(ng)]
                for k0 in range(0, KF, KG):
                    ai = wi % 4
                    wi += 1
                    at = self.view(0, ai * 2048, [128, KG, 512], BF16, name=f"at{ai}")
                    wt = self.view(1, ai * 4096, [128, KG, G * 128], BF16, name=f"wd{ai}")
                    self.dma(at, B(self.act_d.ap()[k0 * 128:(k0 + KG) * 128, tsl].rearrange("(k p) t -> p k t", p=128), Res("t")),
                             extra_r=self.Dall(self.act_d))
                    wv = self.wfull["w_dn", l].ap().rearrange("(k p) n -> p k n", p=128)
                    self.dma(wt[:, :, 0:ng * 128], B(wv[:, k0:k0 + KG, g0 * 128:(g0 + ng) * 128], self.wres("w_dn", l)))
                    for kk in range(KG):
                        k = k0 + kk
                        for i in range(ng):
                            self.mm(accs[i], wt[:, kk, i * 128:(i + 1) * 128], at[:, kk, :], start=(k == 0), stop=(k == KF - 1))
                for i in range(ng):
                    fc = g0 + i
                    ff = self.stage([128, 512], F32)
                    self.cp(ff, accs[i])
                    self.dma(self.D(self.mo_d, (fc, tt), (slice(fc * 128, (fc + 1) * 128), tsl)), ff, q="act")
                    sq = self.stage([128, 512], F32)
                    self.act(sq, ff, AF.Square)
                    self.mm(ssp, self.onesf, sq, start=(fc == 0), stop=(fc == KD - 1))
            rstd = self.view(2, 0, [128, 512], F32, name="rs3")
            self.act(rstd, ssp, AF.Sqrt, scale=1.0 / c.D, bias=self.epsb)
            self.recip(rstd, rstd)
            for fc in range(KD):
                rows = slice(fc * 128, (fc + 1) * 128)
                ff = self.stage([128, 512], F32)
                self.dma(ff, self.D(self.mo_d, (fc, tt), (rows, tsl)))
                xc = self.stage([128, 512], F32)
                self.dma(xc, self.D(self.xmid, (fc, tt), (rows, tsl)))
                self.stt(ff, ff, self.ppc(c.pg + (l * 4 + 3) * KD + fc), rstd, ALU.mult, ALU.mult)
                self.tt(xc, xc, ff, ALU.add)
                o = self.dma(self.D(xdst, (fc, tt), (rows, tsl)), xc, q="act")
                if l == c.L - 1:
                    self.outs.append(o)
            self.fence([2])

    def build(self):
        c = self.c
        self.declare()
        self.alloc()
        self.epsb = self.msc(1)
        self.oneb = self.msc(1)
        self.memset(self.epsb, EPS)
        self.memset(self.oneb, 1.0)
        self.setup()
        allw = ["w_in", "w_ao", "w_co", "w_ho", "w_o", "w_up", "w_dn"]
        import os
        stop = int(os.environ.get("KSTOP", "99"))
        n = 0
        if stop > 0:
            self.weights_prologue(0, allw)
        for l in range(c.L):
            for ph in (self.phaseA, self.phaseCsum, None, self.phaseHgrn1, self.phaseAttn, self.phaseConv,
                       self.phaseHgrn2, self.phaseC, self.phaseD1, self.phaseD2):
                n += 1
                if n >= stop:
                    break
                if ph is None:
                    if l + 1 < c.L:
                        self.weights_prologue(l + 1, allw)
                else:
                    ph(l)
            if n >= stop:
                break
        dn = os.environ.get("KDUMP")
        if dn:
            src = getattr(self, dn)
            dbg = self.dt_("dbg", list(src.ap().shape), src.ap().dtype, "ExternalOutput")
            o = self.dma(B(dbg.ap(), Res("dbg")), B(src.ap(), Res("dsrc")), q="act", extra_r=self.Dall(src))
            self.outs.append(o)
        with self.nc.allow_non_contiguous_dma(reason="small strided scalar/halo transfers"):
            self.R.emit(self.nc, self.es, final_waits=self.outs)
        return self.nc


def host_inputs(cfg, inp):
    c = cfg
    L, NC = c.L, c.NC
    bf = ml_dtypes.bfloat16
    x = np.asarray(inp["x"], np.float32).reshape(c.S, c.D)
    xT = np.ascontiguousarray(x.T)
    wmap = dict(w_in="w_in", w_ao="w_attn_out", w_co="w_conv_out", w_ho="w_hgrn_out", w_o="w_o", w_up="w_ffn_up", w_dn="w_ffn_down")

    def fm(v, nch):
        return np.asarray(v, np.float32).reshape(nch, 128).T

    ng = np.asarray(inp["norm_gains"], np.float32)
    cols = []
    for l in range(L):
        for i in range(4):
            cols.append(fm(ng[l, i], c.KD))
    cw = np.asarray(inp["conv_dw"], np.float32)
    for l in range(L):
        for cj in range(c.KC):
            cols.append(cw[l][:, cj * 128:(cj + 1) * 128].T)
    for key in ("conv_b", "conv_ln_g", "conv_ln_b"):
        a = np.asarray(inp[key], np.float32)
        for l in range(L):
            cols.append(fm(a[l], c.KC))
    for key in ("hgrn_lb_logits", "hgrn_norm_g"):
        a = np.asarray(inp[key], np.float32)
        for l in range(L):
            cols.append(fm(a[l], c.HH))
    fw = np.asarray(inp["ffn_dw"], np.float32)
    for l in range(L):
        for ci in range(2 * c.KF):
            cols.append(fw[l][:, ci * 128:(ci + 1) * 128].T)
    base = np.concatenate(cols, axis=1)
    m64 = np.zeros((128, 64), np.float32)
    m64[:64] = np.triu(np.ones((64, 64), np.float32))
    bfg = np.ascontiguousarray(np.asarray(inp["b_fgate"], np.float32).T)
    ident = np.eye(128, dtype=np.float32)
    kk_, qq_ = np.meshgrid(np.arange(128), np.arange(128), indexing="ij")
    tri = np.where(kk_ > qq_, NEG, 0.0).astype(np.float32)
    ones = np.ones((128, 128), np.float32)
    aq = np.zeros((c.NAUG, c.TPC), np.float32)
    aq[4:8] = 1.0
    aq[8] = 1.0
    qsub = np.arange(c.TPC) // 128
    for i in range(c.NSUB):
        aq[9 + i] = np.where(qsub < i, NEG, 0.0)
    maps = []
    for r in range(NC):
        m = {}
        m["xT"] = np.ascontiguousarray(xT[:, r * c.TPC:(r + 1) * c.TPC])
        for n, key in wmap.items():
            w = np.asarray(inp[key], np.float32)
            if os.environ.get("KNOAG"):
                m[n + "_s"] = w
            else:
                rs = w.shape[1] // NC
                m[n + "_s"] = np.ascontiguousarray(w[:, r * rs:(r + 1) * rs, :])
        mlt = (np.arange(8) < r).astype(np.float32)
        oh = (np.arange(8) == r - 1).astype(np.float32)
        m["pp"] = np.ascontiguousarray(np.concatenate(
            [base, np.tile(mlt, (128, 1)), np.tile(oh, (128, 1)), m64], axis=1).astype(np.float32))
        assert m["pp"].shape[1] == c.NPP
        m["bfg"] = bfg
        own = [ident if j == r else np.zeros((128, 128), np.float32) for j in range(8)]
        m["cb"] = np.concatenate([ident, tri, ones] + own, axis=1).astype(bf)
        ak = np.zeros((c.NAUG, c.S), np.float32)
        ak[0:4] = -1.0
        seg = np.arange(c.S) // c.TPC
        ak[8] = np.where(seg > r, NEG, 0.0)
        kblk = (np.arange(c.S) % c.TPC) // 128
        for i in range(c.NSUB):
            ak[9 + i] = ((seg == r) & (kblk == i)).astype(np.float32)
        m["augKc"] = ak.astype(bf)
        m["augQc"] = aq.astype(bf)
        maps.append(m)
    return maps


_CACHE = {}


def run_cfg(cfg, inp):
    key = (cfg.D, cfg.S, cfg.HA, cfg.CC, cfg.HH, cfg.DFF)
    if key not in _CACHE:
        _CACHE[key] = Builder(cfg).build()
    nc = _CACHE[key]
    maps = host_inputs(cfg, inp)
    res = run_bass_kernel_spmd(nc, maps, core_ids=list(range(cfg.NC)))
    if os.environ.get("KDUMP"):
        return [np.asarray(res.results[r]["dbg"]) for r in range(cfg.NC)]
    outT = np.concatenate([res.results[r]["outT"] for r in range(cfg.NC)], axis=1)
    return np.ascontiguousarray(outT.T).reshape(1, cfg.S, cfg.D).astype(np.float32)


def kernel(**inputs):
    return run_cfg(Cfg(), inputs)
```

```python
import os
import numpy as np
import ml_dtypes
from contextlib import ExitStack
import concourse.bass as bass
import concourse.mybir as mybir
from concourse.bass_utils import run_bass_kernel_spmd

F32 = mybir.dt.float32
BF16 = mybir.dt.bfloat16
AF = mybir.ActivationFunctionType
ALU = mybir.AluOpType
NEG = -30000.0
EPS = 1e-6


class Cfg:
    def __init__(s, D=4096, S=16384, HA=16, CC=1024, HH=8, DFF=11008, L=2, NC=8):
        s.D, s.S, s.HA, s.CC, s.HH, s.DFF, s.L, s.NC = D, S, HA, CC, HH, DFF, L, NC
        s.AW = HA * 128
        s.HW = HH * 128
        s.TPC = S // NC
        s.T = 512
        s.NT = s.TPC // s.T
        s.KD = D // 128
        s.KC = CC // 128
        s.KF = DFF // 128
        s.KB = (s.AW + s.CC + s.HW) // 128
        sp = (s.AW, s.AW, s.AW, HA, 2 * CC, s.HW, s.HW, s.HW, s.HW, D, D, D)
        off = np.concatenate([[0], np.cumsum(sp)]).tolist()
        s.NIN = off[-1]
        (s.oq, s.ok, s.ov, s.ofg, s.oglu, s.ohq, s.ohf, s.ohi, s.ohg, s.oga, s.ogb, s.ogc) = off[:-1]
        s.NSUB = s.TPC // 128
        s.NAUG = 9 + s.NSUB
        c = 0
        s.pg = c; c += L * 4 * s.KD
        s.pcw = c; c += L * s.KC * 31
        s.pcb = c; c += L * s.KC
        s.plg = c; c += L * s.KC
        s.plb = c; c += L * s.KC
        s.plog = c; c += L * HH
        s.phg = c; c += L * HH
        s.pfw = c; c += L * 2 * s.KF * 3
        s.pm = c; c += 8
        s.po = c; c += 8
        s.pm64 = c; c += 64
        s.NPP = c


class Res:
    __slots__ = ("name", "lw", "rdc", "rdd", "parent", "children")

    def __init__(self, name, parent=None):
        self.name = name
        self.lw = None
        self.rdc = {}
        self.rdd = []
        self.parent = parent
        self.children = []
        if parent is not None:
            parent.children.append(self)


class Op:
    __slots__ = ("eng", "fn", "kind", "grp", "deps", "signal", "ticket", "idx")

    def __init__(self, eng, fn, kind, grp):
        self.eng, self.fn, self.kind, self.grp = eng, fn, kind, grp
        self.deps = {}
        self.signal = False
        self.ticket = None


class B:
    __slots__ = ("ap", "res")

    def __init__(self, ap, res):
        self.ap, self.res = ap, res

    def __getitem__(self, k):
        return B(self.ap[k], self.res)

    def rr(self, pat, **kw):
        return B(self.ap.rearrange(pat, **kw), self.res)


class Rec:
    ENGS = ("pe", "act", "dve", "pool", "sp")

    def __init__(self):
        self.ops = {e: [] for e in self.ENGS}
        self.n = 0
        self.dq = {e: [] for e in self.ENGS}
        self.KRING = 16
        import os
        self.limit = int(os.environ.get("KOPS", "100000000"))

    def _dep(self, o, p, raw):
        if p is None or p is o:
            return
        o.deps[p] = o.deps.get(p, False) or raw

    def op(self, eng, fn, r=(), w=(), kind="c", grp=None):
        o = Op(eng, fn, kind, grp)
        o.idx = self.n
        self.n += 1
        if self.n == int(os.environ.get("KPRINT", "-1")):
            import traceback
            traceback.print_stack(limit=5)
            print("KPRINT op", eng, kind, [x.name for x in r], [x.name for x in w])
        if self.n > self.limit:
            o.ticket = ("Esp", 0)
            return o
        if kind == "dma":
            q = self.dq[eng]
            o.grp = f"{eng}{len(q) % self.KRING}"
            if len(q) >= self.KRING:
                self._dep(o, q[len(q) - self.KRING], False)
            q.append(o)
        for res in r:
            self._dep(o, res.lw, True)
            if res.parent is not None:
                self._dep(o, res.parent.lw, True)
            for ch in res.children:
                self._dep(o, ch.lw, True)
        for res in w:
            chain = [res] + ([res.parent] if res.parent is not None else []) + list(res.children)
            for q in chain:
                self._dep(o, q.lw, False)
                for p in q.rdc.values():
                    self._dep(o, p, False)
                for p in q.rdd:
                    self._dep(o, p, False)
        for res in r:
            if kind == "c":
                res.rdc[eng] = o
            else:
                res.rdd.append(o)
        for res in w:
            res.lw = o
            res.rdc = {}
            res.rdd = []
            for ch in res.children:
                ch.lw = None
                ch.rdc = {}
                ch.rdd = []
        self.ops[eng].append(o)
        return o

    def finalize(self):
        for e in self.ENGS:
            for o in self.ops[e]:
                keep = {}
                for p, raw in o.deps.items():
                    if p.kind == "c" and o.kind == "c" and p.eng == o.eng:
                        if o.eng == "pe" or not raw:
                            continue
                    keep[p] = raw
                    p.signal = True
                o.deps = keep
        self.groups = {}
        for e in self.ENGS:
            cnt = 0
            for o in self.ops[e]:
                if o.kind == "c":
                    if o.signal:
                        cnt += 1
                        o.ticket = ("E" + e, cnt)
                else:
                    inc = 16 if o.kind == "dma" else 1
                    g = self.groups.get(o.grp, 0) + inc
                    self.groups[o.grp] = g
                    o.ticket = ("G" + o.grp, g)

    def emit(self, nc, es, final_waits=()):
        self.finalize()
        sems = {}

        def sem(key):
            if key not in sems:
                sems[key] = es.enter_context(nc.semaphore(key[:24] + str(len(sems))))
            return sems[key]

        for e in self.ENGS:
            sem("E" + e)
        for g in self.groups:
            sem("G" + g)
        self.nsem = len(sems)

        def run(e, eng):
            seen = {}
            for o in self.ops[e]:
                need = {}
                for p in o.deps:
                    k, v = p.ticket
                    if v > need.get(k, 0):
                        need[k] = v
                for k, v in need.items():
                    if v > seen.get(k, 0):
                        eng.wait_ge(sem(k), v)
                        seen[k] = v
                inst = o.fn(eng)
                if o.kind == "c":
                    if o.signal:
                        inst.then_inc(sem(o.ticket[0]), 1)
                elif o.kind == "dma":
                    inst.then_inc(sem(o.ticket[0]), 16)
                else:
                    inst.then_inc(sem(o.ticket[0]))
            if e == "sp":
                for p in final_waits:
                    k, v = p.ticket
                    eng.wait_ge(sem(k), v)

        with nc.Block() as block:
            block.tensor(lambda eng: run("pe", eng))
            block.scalar(lambda eng: run("act", eng))
            block.vector(lambda eng: run("dve", eng))
            block.gpsimd(lambda eng: run("pool", eng))
            block.sync(lambda eng: run("sp", eng))


class Builder:
    def __init__(self, cfg):
        self.c = cfg
        self.nc = bass.Bass("TRN2", target_bir_lowering=False)
        self.R = Rec()
        self.es = ExitStack()
        self.dres = {}
        self.outs = []
        self.rot = {}

    def dt_(self, name, shape, dtype, kind="Internal"):
        return self.nc.dram_tensor(name, list(shape), dtype, kind=kind)

    def D(self, t, key, idx):
        k = (t.name, key)
        if k not in self.dres:
            self.dres[k] = Res(f"{t.name}:{key}")
        return B(t.ap()[idx], self.dres[k])

    def Dall(self, t):
        return [r for (n, _), r in self.dres.items() if n == t.name]

    def dma(self, out, in_, q="sp", extra_r=(), extra_w=()):
        o_ap, i_ap = out.ap, in_.ap
        return self.R.op(q, lambda e: e.dma_start(out=o_ap, in_=i_ap), r=[in_.res] + list(extra_r),
                         w=[out.res] + list(extra_w), kind="dma")

    def mm(self, out, lhsT, rhs, start, stop):
        o, l, r_ = out.ap, lhsT.ap, rhs.ap
        return self.R.op("pe", lambda e: e.matmul(o, l, r_, start=start, stop=stop), r=[lhsT.res, rhs.res], w=[out.res])

    def tr(self, out, in_, ident):
        o, i, d = out.ap, in_.ap, ident.ap
        return self.R.op("pe", lambda e: e.transpose(o, i, d), r=[in_.res, ident.res], w=[out.res])

    def act(self, out, in_, func, scale=1.0, bias=0.0, eng="act"):
        rs = [in_.res]
        sc = scale
        bi = bias
        if isinstance(scale, B):
            rs.append(scale.res)
            sc = scale.ap
        if isinstance(bias, B):
            rs.append(bias.res)
            bi = bias.ap
        o, i = out.ap, in_.ap
        return self.R.op("act", lambda e: e.activation(out=o, in_=i, func=func, bias=bi, scale=sc), r=rs, w=[out.res])

    def tt(self, out, a, b, op, eng="dve"):
        o, x, y = out.ap, a.ap, b.ap
        return self.R.op(eng, lambda e: e.tensor_tensor(out=o, in0=x, in1=y, op=op), r=[a.res, b.res], w=[out.res])

    def ts(self, out, a, s1, s2=None, op0=ALU.mult, op1=None, eng="dve"):
        rs = [a.res]
        v1, v2 = s1, s2
        if isinstance(s1, B):
            rs.append(s1.res)
            v1 = s1.ap
        if isinstance(s2, B):
            rs.append(s2.res)
            v2 = s2.ap
        o, x = out.ap, a.ap
        if op1 is None:
            return self.R.op(eng, lambda e: e.tensor_scalar(out=o, in0=x, scalar1=v1, scalar2=None, op0=op0), r=rs, w=[out.res])
        return self.R.op(eng, lambda e: e.tensor_scalar(out=o, in0=x, scalar1=v1, scalar2=v2, op0=op0, op1=op1), r=rs, w=[out.res])

    def stt(self, out, a, s, b, op0, op1):
        rs = [a.res, b.res]
        v = s
        if isinstance(s, B):
            rs.append(s.res)
            v = s.ap
        o, x, y = out.ap, a.ap, b.ap
        return self.R.op("dve", lambda e: e.scalar_tensor_tensor(out=o, in0=x, scalar=v, in1=y, op0=op0, op1=op1), r=rs, w=[out.res])

    def cp(self, out, in_, eng="dve"):
        o, i = out.ap, in_.ap
        if eng == "act":
            return self.R.op("act", lambda e: e.copy(out=o, in_=i), r=[in_.res], w=[out.res])
        return self.R.op(eng, lambda e: e.tensor_copy(out=o, in_=i), r=[in_.res], w=[out.res])

    def recip(self, out, in_):
        o, i = out.ap, in_.ap
        return self.R.op("dve", lambda e: e.reciprocal(out=o, in_=i), r=[in_.res], w=[out.res])

    def memset(self, out, val, eng="dve"):
        o = out.ap
        return self.R.op(eng, lambda e: e.memset(o, val), r=[], w=[out.res])

    def scan(self, out, d0, d1, init=0.0):
        o, a, b = out.ap, d0.ap, d1.ap
        return self.R.op("dve", lambda e: e.tensor_tensor_scan(out=o, data0=a, data1=b, initial=init, op0=ALU.mult, op1=ALU.add),
                         r=[d0.res, d1.res], w=[out.res])

    def ag(self, out_t, in_t, r, w, name):
        nranks = self.c.NC
        import os
        if os.environ.get("KNOAG"):
            nr = in_t.ap().shape[0]
            reps = out_t.ap().shape[0] // nr
            o = None
            for i in range(reps):
                o = self.R.op("pool", lambda e, i=i: e.dma_start(out=out_t.ap()[i * nr:(i + 1) * nr], in_=in_t.ap()), r=r, w=w, kind="dma")
            return o
        if not hasattr(self, "ccres"):
            self.ccres = Res("ccchain")
            self.agn = 0
        R_, C_ = list(in_t.ap().shape)
        esz = 4 if in_t.ap().dtype == F32 else 2
        nr = max(1, min(R_, (512 * 1024) // (C_ * esz)))
        key = (nr, C_, esz)
        if not hasattr(self, "agtmp"):
            self.agtmp = {}
        if key not in self.agtmp:
            i = len(self.agtmp)
            self.agtmp[key] = (self.dt_(f"agm{i}", [4 * nr, C_], in_t.ap().dtype),
                               [self.dt_(f"ago{i}_{k}", [8 * nr, C_], in_t.ap().dtype) for k in range(2)])
        mid, outs2 = self.agtmp[key]
        midres = self.D(mid, "all", (slice(None), slice(None))).res
        last = None
        for r0 in range(0, R_, nr):
            n = min(nr, R_ - r0)
            o2 = outs2[self.agn % 2]
            self.agn += 1
            o2res = self.D(o2, "all", (slice(None), slice(None))).res
            i_ap = in_t.ap()[r0:r0 + n, :].opt()
            m_ap = mid.ap()[0:4 * n, :].opt()
            o_ap = o2.ap()[0:8 * n, :].opt()
            self.R.op("pool", lambda e, i_ap=i_ap, m_ap=m_ap: e.collective_compute(
                "AllGather", ALU.bypass, replica_groups=[[0, 1, 2, 3], [4, 5, 6, 7]], ins=[i_ap], outs=[m_ap]),
                r=r, w=[midres, self.ccres], kind="cc", grp="AG1")
            self.R.op("pool", lambda e, m_ap=m_ap, o_ap=o_ap: e.collective_compute(
                "AllGather", ALU.bypass, replica_groups=[[0, 4], [1, 5], [2, 6], [3, 7]], ins=[m_ap], outs=[o_ap]),
                r=[midres], w=[o2res, self.ccres], kind="cc", grp="AG2")
            dst = out_t.ap().rearrange("(j r) c -> j r c", j=8)[:, r0:r0 + n, :]
            src = o2.ap()[0:8 * n, :].rearrange("(j r) c -> j r c", j=8)
            last = self.R.op("pool", lambda e, dst=dst, src=src: e.dma_start(out=dst, in_=src), r=[o2res], w=w, kind="dma")
        return last

    def alloc(self):
        c = self.c
        nc = self.nc
        es = self.es
        self.BIG = 32768
        self.big = []
        for i in range(5):
            t = es.enter_context(nc.sbuf_tensor(f"big{i}", [128, self.BIG // 2], BF16))
            self.big.append((t, Res(f"big{i}")))
        self.stg = []
        for i in range(8):
            t = es.enter_context(nc.sbuf_tensor(f"stg{i}", [128, 1040], BF16))
            self.stg.append((t, Res(f"stg{i}")))
        self.pp_t = es.enter_context(nc.sbuf_tensor("sb_pp", [128, c.NPP], F32))
        self.pp = B(self.pp_t[:], Res("pp"))
        self.cb_t = es.enter_context(nc.sbuf_tensor("sb_cb", [128, 128 * 11], BF16))
        self.cb = B(self.cb_t[:], Res("cb"))
        self.onesf_t = es.enter_context(nc.sbuf_tensor("onesf", [128, 128], F32))
        self.onesf = B(self.onesf_t[:], Res("onesf"))
        self.ones_t = es.enter_context(nc.sbuf_tensor("onesb", [128, c.TPC], BF16))
        self.onesb = B(self.ones_t[:], Res("onesb"))
        self.misc_t = es.enter_context(nc.sbuf_tensor("misc", [128, 1024], F32))
        self.misc = Res("misc")
        self.psum = []
        for i in range(8):
            t = es.enter_context(nc.psum_tensor(f"ps{i}", [128, 512], F32))
            self.psum.append(B(t[:], Res(f"ps{i}")))

    def view(self, slot, off, shape, dtype, name=None, pool=None):
        t, res = (pool or self.big)[slot]
        esz = 4 if dtype == F32 else 2
        n = int(np.prod(shape[1:]))
        ap = t[0:shape[0], off // 2: off // 2 + n * esz // 2]
        if dtype == F32:
            ap = ap.bitcast(F32)
        if len(shape) == 3:
            ap = ap.rearrange("p (a b) -> p a b", b=shape[2])
        if not hasattr(self, "vcache"):
            self.vcache = {}
        if name is None:
            return B(ap, Res(f"{res.name}@{off}", parent=res))
        k = (res.name, name)
        if k not in self.vcache:
            self.vcache[k] = (Res(name, parent=res), off, n * esz)
        assert self.vcache[k][1:] == (off, n * esz), (k, self.vcache[k][1:], off, n * esz)
        return B(ap, self.vcache[k][0])

    def fence(self, slots):
        for s in slots:
            t, res = self.big[s]
            ap = t[0:1, 0:2]
            self.R.op("dve", lambda e, ap=ap: e.memset(ap, 0.0), r=[], w=[res])
            res.children = []
            if hasattr(self, "vcache"):
                for k in [k for k in self.vcache if k[0] == res.name]:
                    del self.vcache[k]

    def stage(self, shape, dtype):
        i = self.rot.get("stg", 0)
        self.rot["stg"] = i + 1
        t, res = self.stg[i % len(self.stg)]
        esz = 4 if dtype == F32 else 2
        n = int(np.prod(shape[1:]))
        ap = t[0:shape[0], 0: n * esz // 2]
        if dtype == F32:
            ap = ap.bitcast(F32)
        return B(ap, res)

    def ps(self, banks):
        key = "ps" + str(banks)
        i = self.rot.get(key, 0)
        self.rot[key] = i + 1
        return self.psum[banks[i % len(banks)]]

    def mi(self, off, n, parts=128):
        return B(self.misc_t[0:parts, off:off + n], self.misc)

    def whole(self, slot, shape, dtype):
        t, res = self.big[slot]
        esz = 4 if dtype == F32 else 2
        n = int(np.prod(shape[1:]))
        ap = t[0:shape[0], 0: n * esz // 2]
        if dtype == F32:
            ap = ap.bitcast(F32)
        if len(shape) == 3:
            ap = ap.rearrange("p (a b) -> p a b", b=shape[2])
        return B(ap, res)

    def msc(self, n, parts=128, name=None):
        if not hasattr(self, "_mc"):
            self._mc = {}
        if name is not None and name in self._mc:
            return self._mc[name]
        o = getattr(self, "_mo", 0)
        self._mo = o + n
        assert self._mo <= 1024, self._mo
        b = B(self.misc_t[0:parts, o:o + n], Res(f"misc{o}"))
        if name is not None:
            self._mc[name] = b
        return b

    def ppc(self, off, n=1):
        return self.pp[:, off:off + n]

    def declare(self):
        c = self.c
        L = c.L
        X = "ExternalInput"
        self.xT = self.dt_("xT", [c.D, c.TPC], F32, X)
        self.wshapes = dict(w_in=(c.D, c.NIN), w_ao=(c.AW, c.D), w_co=(c.CC, c.D), w_ho=(c.HW, c.D),
                            w_o=(c.D, c.D), w_up=(c.D, 2 * c.DFF), w_dn=(c.DFF, c.D))
        self.wext, self.wsh, self.wfull = {}, {}, {}
        self.wdiv = 1 if os.environ.get("KNOAG") else c.NC
        for n, (r, k) in self.wshapes.items():
            self.wext[n] = self.dt_(n + "_s", [L, r // self.wdiv, k], F32, X)
            for l in range(L):
                self.wsh[n, l] = self.dt_(f"{n}_sh{l}", [r // self.wdiv, k], BF16)
                self.wfull[n, l] = self.dt_(f"{n}_f{l}", [r, k], BF16)
        self.pp_d = self.dt_("pp", [128, c.NPP], F32, X)
        self.bfg_d = self.dt_("bfg", [c.HA, L], F32, X)
        self.cb_d = self.dt_("cb", [128, 128 * 11], BF16, X)
        self.augKc = self.dt_("augKc", [c.NAUG, c.S], BF16, X)
        self.augQc = self.dt_("augQc", [c.NAUG, c.TPC], BF16, X)
        self.outT = self.dt_("outT", [c.D, c.TPC], F32, "ExternalOutput")
        I = self.dt_
        self.x1 = I("x1", [c.D, c.TPC], F32)
        self.xmid = I("xmid", [c.D, c.TPC], F32)
        self.mo_d = I("mo_d", [c.D, c.TPC], F32)
        self.hT_d = I("hT_d", [c.D, c.TPC], BF16)
        self.h2_d = I("h2_d", [c.D, c.TPC], BF16)
        self.act_d = I("act_d", [c.DFF, c.TPC], BF16)
        self.brT_d = I("brT_d", [c.KB * 128, c.TPC], BF16)
        self.qT_d = I("qT_d", [c.AW, c.TPC], BF16)
        self.kT_loc = I("kT_loc", [c.AW, c.TPC], BF16)
        self.kT_all = I("kT_all", [c.NC * c.AW, c.TPC], BF16)
        self.v_loc = I("v_loc", [c.TPC, c.AW], BF16)
        self.v_all = I("v_all", [c.S, c.AW], BF16)
        self.cag_loc = I("cag_loc", [c.HA, c.TPC], F32)
        self.cag_all = I("cag_all", [c.NC * c.HA, c.TPC], F32)
        self.augK_d = I("augK_d", [c.HA, c.NAUG, c.S], BF16)
        self.augQ_d = I("augQ_d", [c.HA, c.NAUG, c.TPC], BF16)
        self.u_d = I("u_d", [c.CC, c.TPC], F32)
        self.uh_loc = I("uh_loc", [c.CC, 32], F32)
        self.uh_all = I("uh_all", [c.NC * c.CC, 32], F32)
        self.hq_d = I("hq_d", [c.HW, c.TPC], F32)
        self.lf_d = I("lf_d", [c.HW, c.TPC], F32)
        self.kk_d = I("kk_d", [c.HW, c.TPC], F32)
        self.hgt_d = I("hgt_d", [c.HW, c.TPC], F32)
        self.hv_d = I("hv_d", [c.TPC, c.HW], BF16)
        self.ol_d = I("ol_d", [c.HW, c.TPC], F32)
        self.qg_d = I("qg_d", [c.HW, c.TPC], BF16)
        self.hst_loc = I("hst_loc", [c.HW, 136], F32)
        self.hst_all = I("hst_all", [c.NC * c.HW, 136], F32)
        self.h2h_loc = I("h2h_loc", [c.D, 16], BF16)
        self.h2h_all = I("h2h_all", [c.NC * c.D, 16], BF16)

    def weights_prologue(self, l, names):
        c = self.c
        for n in names:
            r, k = self.wshapes[n]
            rs = r // self.wdiv
            nblk = max(1, min(8, rs // 16))
            step = rs // nblk
            for b in range(nblk):
                sl = slice(b * step, (b + 1) * step if b < nblk - 1 else rs)
                src = B(self.wext[n].ap()[l, sl, :], Res("wext"))
                dst = self.D(self.wsh[n, l], b, (sl, slice(None)))
                self.dma(dst, src, q="pool")
            self.ag(self.wfull[n, l], self.wsh[n, l], r=self.Dall(self.wsh[n, l]),
                    w=[self.D(self.wfull[n, l], "all", (slice(None), slice(None))).res], name=n)

    def wres(self, n, l):
        return self.D(self.wfull[n, l], "all", (slice(None), slice(None))).res

    def setup(self):
        c = self.c
        self.dma(self.pp, B(self.pp_d.ap(), Res("ppd")))
        self.dma(self.cb, B(self.cb_d.ap(), Res("cbd")))
        self.memset(self.onesf, 1.0)
        self.memset(self.onesb, 1.0)
        self.ident = self.cb[:, 0:128]
        self.tri = self.cb[:, 128:256]
        self.ones128 = self.cb[:, 256:384]
        self.negb = self.msc(c.L, c.HA)
        bf = self.msc(c.L, c.HA)
        self.dma(bf, B(self.bfg_d.ap(), Res("bfgd")))
        self.ts(self.negb, bf, -1.0)
        HH = c.HH
        self.lb = self.msc(c.L * HH)
        self.oml = self.msc(c.L * HH)
        self.noml = self.msc(c.L * HH)
        self.memset(self.lb, 0.0)
        d = self.msc(HH)
        self.tt(d, self.ppc(c.plog + HH, HH), self.ppc(c.plog, HH), ALU.subtract)
        self.act(self.lb[:, HH:2 * HH], d, AF.Sigmoid)
        self.ts(self.oml, self.lb, -1.0, 1.0, ALU.mult, ALU.add)
        self.ts(self.noml, self.oml, -1.0)
        for h in range(c.HA):
            self.dma(self.D(self.augK_d, ("c", h), (h, slice(None), slice(None))), B(self.augKc.ap(), Res("akc")), q="act")
            self.dma(self.D(self.augQ_d, ("c", h), (h, slice(None), slice(None))), B(self.augQc.ap(), Res("aqc")), q="act")

    def rms_stats(self, src_fn, nchunk, Dn):
        ssp = self.psum[7]
        for kc in range(nchunk):
            xc = src_fn(kc)
            sq = self.stage([128, 512], F32)
            self.act(sq, xc, AF.Square)
            self.mm(ssp, self.onesf, sq, start=(kc == 0), stop=(kc == nchunk - 1))
        rstd = self.view(4, 16384, [128, 512], F32, name="rstdA")
        self.act(rstd, ssp, AF.Sqrt, scale=1.0 / Dn, bias=self.epsb)
        self.recip(rstd, rstd)
        return rstd

    def ldw(self, slot, n, l, ranges, nk):
        W = self.wfull[n, l]
        wv = W.ap().rearrange("(k p) n -> p k n", p=128)
        ncols = sum(r[1] for r in ranges)
        wt = self.whole(slot, [128, nk, ncols], BF16)
        o = 0
        for (c0, nn) in ranges:
            self.dma(wt[:, :, o:o + nn], B(wv[:, :, c0:c0 + nn], self.wres(n, l)))
            o += nn
        return wt

    def phaseA(self, l):
        c = self.c
        KD = c.KD
        xsrc = self.xT if l == 0 else self.x1
        lnv = self.view(4, 0, [c.HA, c.TPC], F32, name="lnv")
        evi = [0]
        wi = [0]

        def wslot():
            wi[0] += 1
            return 2 + (wi[0] % 2)

        abanks = [0, 1, 2, 3, 4, 5]
        for tt in range(c.NT):
            tsl = slice(tt * 512, (tt + 1) * 512)
            hT = self.whole(tt % 2, [128, KD, 512], BF16)

            def xchunk(kc):
                xc = self.stage([128, 512], F32)
                self.dma(xc, self.D(xsrc, (kc, tt), (slice(kc * 128, (kc + 1) * 128), tsl)))
                return xc
            rstd = self.rms_stats(xchunk, KD, c.D)
            for kc in range(KD):
                xc = xchunk(kc)
                self.stt(hT[:, kc, :], xc, self.ppc(c.pg + (l * 4 + 0) * KD + kc), rstd, ALU.mult, ALU.mult)
            self.dma(self.D(self.hT_d, tt, (slice(None), tsl)).rr("(k p) t -> p k t", p=128), hT, q="act")

            def fm(wt, col, M):
                psb = self.ps(abanks)
                for kc in range(KD):
                    self.mm(psb[0:M, :], wt[:, kc, col:col + M], hT[:, kc, :], start=(kc == 0), stop=(kc == KD - 1))
                return psb

            def tm(wt, ncols, dst_t, c0):
                for t4 in range(4):
                    psb = self.ps(abanks)
                    for kc in range(KD):
                        self.mm(psb[:, 0:ncols], hT[:, kc, t4 * 128:(t4 + 1) * 128], wt[:, kc, 0:ncols],
                                start=(kc == 0), stop=(kc == KD - 1))
                    st = self.stage([128, ncols], BF16)
                    evi[0] += 1
                    self.cp(st, psb[:, 0:ncols], eng=("act" if evi[0] % 2 else "dve"))
                    r0 = tt * 512 + t4 * 128
                    self.dma(self.D(dst_t, (tt, t4, c0), (slice(r0, r0 + 128), slice(c0, c0 + ncols))), st, q="act")

            def store_fm(st, dst_t, row0, M=128):
                self.dma(self.D(dst_t, (row0, tt), (slice(row0, row0 + M), tsl)), st, q="act")

            for (off, dst, isq) in ((c.oq, self.qT_d, True), (c.ok, self.kT_loc, False)):
                for g0 in range(0, c.AW, 512):
                    n = min(512, c.AW - g0)
                    wt = self.ldw(wslot(), "w_in", l, [(off + g0, n)], KD)
                    for j in range(n // 128):
                        psb = fm(wt, j * 128, 128)
                        st = self.stage([128, 512], BF16)
                        if isq:
                            self.act(st, psb, AF.Copy, scale=128.0 ** -0.5)
                        else:
                            self.cp(st, psb)
                        store_fm(st, dst, g0 + j * 128)
            for g0 in range(0, c.AW, 512):
                n = min(512, c.AW - g0)
                wt = self.ldw(wslot(), "w_in", l, [(c.ov + g0, n)], KD)
                tm(wt, n, self.v_loc, g0)
            wt = self.ldw(wslot(), "w_in", l, [(c.ofg, c.HA)], KD)
            psb = fm(wt, 0, c.HA)
            e1 = self.stage([c.HA, 512], F32)
            self.act(e1, psb[0:c.HA, :], AF.Exp, scale=-1.0, bias=self.negb[:, l:l + 1])
            self.act(lnv[:, tsl], e1, AF.Ln, bias=self.oneb[0:c.HA, :])
            gcols = min(256, c.CC)
            for g0 in range(0, c.CC, gcols):
                wt = self.ldw(wslot(), "w_in", l, [(c.oglu + g0, gcols), (c.oglu + c.CC + g0, gcols)], KD)
                for j in range(gcols // 128):
                    pv = fm(wt, j * 128, 128)
                    pg = fm(wt, gcols + j * 128, 128)
                    sg = self.stage([128, 512], F32)
                    self.act(sg, pg, AF.Sigmoid)
                    u = self.stage([128, 512], F32)
                    self.tt(u, pv, sg, ALU.mult)
                    store_fm(u, self.u_d, g0 + j * 128)
            for g0 in range(0, c.HW, 512):
                n = min(512, c.HW - g0)
                wt = self.ldw(wslot(), "w_in", l, [(c.ohq + g0, n)], KD)
                for j in range(n // 128):
                    psb = fm(wt, j * 128, 128)
                    st = self.stage([128, 512], F32)
                    self.act(st, psb, AF.Silu)
                    store_fm(st, self.hq_d, g0 + j * 128)
            for g0 in range(0, c.HW, 512):
                n = min(512, c.HW - g0)
                wt = self.ldw(wslot(), "w_in", l, [(c.ohf + g0, n)], KD)
                for j in range(n // 128):
                    hh = (g0 + j * 128) // 128
                    psb = fm(wt, j * 128, 128)
                    sg = self.stage([128, 512], F32)
                    self.act(sg, psb, AF.Sigmoid)
                    lf = self.stage([128, 512], F32)
                    ci = l * c.HH + hh
                    self.act(lf, sg, AF.Ln, scale=self.oml[:, ci:ci + 1], bias=self.lb[:, ci:ci + 1])
                    store_fm(lf, self.lf_d, g0 + j * 128)
                    kk = self.stage([128, 512], F32)
                    self.ts(kk, sg, self.noml[:, ci:ci + 1], self.oml[:, ci:ci + 1], ALU.mult, ALU.add)
                    store_fm(kk, self.kk_d, g0 + j * 128)
            for g0 in range(0, c.HW, 512):
                n = min(512, c.HW - g0)
                wt = self.ldw(wslot(), "w_in", l, [(c.ohi + g0, n)], KD)
                tm(wt, n, self.hv_d, g0)
            for g0 in range(0, c.HW, 512):
                n = min(512, c.HW - g0)
                wt = self.ldw(wslot(), "w_in", l, [(c.ohg + g0, n)], KD)
                for j in range(n // 128):
                    psb = fm(wt, j * 128, 128)
                    st = self.stage([128, 512], F32)
                    self.act(st, psb, AF.Silu)
                    store_fm(st, self.hgt_d, g0 + j * 128)
        self.dma(self.D(self.uh_loc, 0, (slice(None), slice(None))),
                 B(self.u_d.ap()[:, c.TPC - 32:c.TPC], Res("tmp")), q="act", extra_r=self.Dall(self.u_d))
        cpl = self.view(4, 8192, [c.HA, c.TPC], F32, name="cpl")
        self.scan(cpl, self.onesb[0:c.HA, :], lnv)
        self.dma(self.D(self.cag_loc, 0, (slice(None), slice(None))), cpl, q="act")
        self.cpl = cpl
        self.ag(self.cag_all, self.cag_loc, r=self.Dall(self.cag_loc), w=[self.D(self.cag_all, "all", (slice(None), slice(None))).res], name="c")
        self.ag(self.uh_all, self.uh_loc, r=self.Dall(self.uh_loc), w=[self.D(self.uh_all, "all", (slice(None), slice(None))).res], name="uh")
        self.ag(self.kT_all, self.kT_loc, r=self.Dall(self.kT_loc), w=[self.D(self.kT_all, "all", (slice(None), slice(None))).res], name="k")
        self.ag(self.v_all, self.v_loc, r=self.Dall(self.v_loc), w=[self.D(self.v_all, "all", (slice(None), slice(None))).res], name="v")

    def allres(self, t):
        return self.D(t, "all", (slice(None), slice(None))).res

    def pieces(self, cur, dst_fn):
        c = self.c
        for h0 in range(0, c.TPC, 1024):
            hs = slice(h0, h0 + 1024)
            for i in range(4):
                p = self.stage([c.HA, 1024], BF16)
                self.cp(p, cur[:, hs])
                dst_fn(i, p, hs)
                if i < 3:
                    self.tt(cur[:, hs], cur[:, hs], p, ALU.subtract)

    def phaseCsum(self, l):
        c = self.c
        HA = c.HA
        self.fence([0, 1])
        tot = self.msc(8, HA, "tot")
        self.dma(tot.rr("h (j o) -> h j o", o=1),
                 B(self.cag_all.ap().rearrange("(j h) t -> h j t", h=HA)[:, :, c.TPC - 1:c.TPC], self.allres(self.cag_all)))
        incl = self.msc(8, HA, "incl")
        self.scan(incl, self.onesb[0:HA, 0:8], tot)
        offs = self.msc(8, HA, "offs")
        self.tt(offs, incl, tot, ALU.subtract)
        tm_ = self.msc(8, HA, "tm_")
        self.tt(tm_, tot, self.pp[0:HA, c.pm:c.pm + 8], ALU.mult)
        inclm = self.msc(8, HA, "inclm")
        self.scan(inclm, self.onesb[0:HA, 0:8], tm_)
        for j in range(c.NC):
            cs = self.view(j % 2, 0, [HA, c.TPC], F32, name=f"cseg{j%2}")
            self.dma(cs, B(self.cag_all.ap()[j * HA:(j + 1) * HA, :], self.allres(self.cag_all)))
            self.ts(cs, cs, offs[:, j:j + 1], None, ALU.add)
            self.pieces(cs, lambda i, p, hs, j=j: self.dma(
                self.D(self.augK_d, ("p", i, j, hs.start), (slice(None), 4 + i, slice(j * c.TPC + hs.start, j * c.TPC + hs.stop))), p, q="act"))
        cq = self.view(0, 8192, [HA, c.TPC], F32, name="cq")
        self.dma(cq, B(self.cag_loc.ap(), Res("t")), extra_r=self.Dall(self.cag_loc))
        self.ts(cq, cq, inclm[:, 7:8], None, ALU.add)
        self.pieces(cq, lambda i, p, hs: self.dma(self.D(self.augQ_d, ("p", i, hs.start), (slice(None), i, hs)), p, q="act"))

    def phaseAttn(self, l):
        c = self.c
        self.fence([0, 1, 2, 3])
        NS = c.NSUB
        qTs = [self.view(0, i * 4096, [128, c.TPC], BF16, name=f"qT{i}") for i in range(2)]
        aQs = [self.view(0, 8192 + i * 4096, [c.NAUG, c.TPC], BF16, name=f"aQ{i}") for i in range(2)]
        PTs = [self.view(0, 16384 + i * 1024, [128, 512], BF16, name=f"PT{i}") for i in range(4)]
        kss = [self.view(1, i * 4096, [128, c.TPC], BF16, name=f"ks{i}") for i in range(4)]
        aKs = [self.view(1, 16384 + i * 4096, [c.NAUG, c.TPC], BF16, name=f"aK{i}") for i in range(4)]
        vss = [self.view(2, i * 4096, [128, NS, 128], BF16, name=f"vs{i}") for i in range(4)]
        kall, vall = self.allres(self.kT_all), self.allres(self.v_all)
        si = 0
        pti = 0
        for h in range(c.HA):
            qT, aQ = qTs[h % 2], aQs[h % 2]
            self.dma(qT, B(self.qT_d.ap()[h * 128:(h + 1) * 128, :], Res("t")), extra_r=self.Dall(self.qT_d))
            self.dma(aQ, B(self.augQ_d.ap()[h], Res("t")), extra_r=self.Dall(self.augQ_d))
            for qb in range(c.NT):
                qsl = slice(qb * 512, (qb + 1) * 512)
                OT = self.psum[3 + (h * c.NT + qb) % 2]
                DN = self.psum[5 + (h * c.NT + qb) % 2]
                for j in range(c.NC):
                    ks, aK, vs = kss[si % 4], aKs[si % 4], vss[si % 4]
                    si += 1
                    r0 = j * c.AW + h * 128
                    self.dma(ks, B(self.kT_all.ap()[r0:r0 + 128, :], kall))
                    self.dma(aK, B(self.augK_d.ap()[h, :, j * c.TPC:(j + 1) * c.TPC], Res("t")), extra_r=self.Dall(self.augK_d))
                    self.dma(vs, B(self.v_all.ap()[j * c.TPC:(j + 1) * c.TPC, h * 128:(h + 1) * 128].rearrange("(b p) d -> p b d", p=128), vall))
                    for kb in range(NS):
                        S_ = self.ps([0, 1, 2])
                        ksl = slice(kb * 128, (kb + 1) * 128)
                        diag = (kb // 4 == qb)
                        self.mm(S_, ks[:, ksl], qT[:, qsl], start=True, stop=False)
                        self.mm(S_, aK[:, ksl], aQ[:, qsl], start=False, stop=not diag)
                        if diag:
                            sub = kb - 4 * qb
                            self.mm(S_[:, sub * 128:(sub + 1) * 128], self.cb[:, (3 + j) * 128:(4 + j) * 128], self.tri,
                                    start=False, stop=True)
                        PT = PTs[pti % 4]
                        pti += 1
                        self.act(PT, S_, AF.Exp)
                        first = (j == 0 and kb == 0)
                        last = (j == c.NC - 1 and kb == NS - 1)
                        self.mm(OT, vs[:, kb, :], PT, start=first, stop=last)
                        self.mm(DN, self.ones128, PT, start=first, stop=last)
                rd = self.stage([128, 512], F32)
                self.recip(rd, DN)
                at = self.stage([128, 512], BF16)
                self.tt(at, OT, rd, ALU.mult)
                self.dma(self.D(self.brT_d, ("a", h, qb), (slice(h * 128, (h + 1) * 128), qsl)), at, q="act")

    def phaseConv(self, l):
        c = self.c
        KC = c.KC
        self.fence([0, 1, 2, 3])
        uh = self.view(0, 0, [128, c.NC * KC, 32], F32, name="uh")
        self.dma(uh, B(self.uh_all.ap().rearrange("(j k p) t -> p (j k) t", p=128, k=KC), self.allres(self.uh_all)))
        halo = self.view(0, 16384, [128, KC, 32], F32, name="halo")
        for j in range(c.NC):
            src = uh[:, j * KC:(j + 1) * KC, :]
            if j == 0:
                self.ts(halo, src, self.ppc(c.po + j), None, ALU.mult)
            else:
                self.stt(halo, src, self.ppc(c.po + j), halo, ALU.mult, ALU.add)
        for tt in range(c.NT):
            tsl = slice(tt * 512, (tt + 1) * 512)
            y = self.whole(1, [128, KC, 512], F32)
            for cj in range(KC):
                ub = self.view(2, (cj % 2) * 4096, [128, 544], F32, name=f"ub{cj%2}")
                rows = slice(cj * 128, (cj + 1) * 128)
                if tt == 0:
                    self.cp(ub[:, 0:32], halo[:, cj, :], eng="act")
                    self.dma(ub[:, 32:544], B(self.u_d.ap()[rows, tsl], Res("t")), extra_r=self.Dall(self.u_d))
                else:
                    self.dma(ub, B(self.u_d.ap()[rows, tt * 512 - 32:(tt + 1) * 512], Res("t")), extra_r=self.Dall(self.u_d))
                wb = c.pcw + (l * KC + cj) * 31
                acc = y[:, cj, :]
                self.ts(acc, ub[:, 2:2 + 512], self.ppc(wb), self.ppc(c.pcb + l * KC + cj), ALU.mult, ALU.add)
                for k in range(1, 31):
                    self.stt(acc, ub[:, 2 + k:2 + k + 512], self.ppc(wb + k), acc, ALU.mult, ALU.add)
            s1, s2 = self.psum[0], self.psum[1]
            for cj in range(KC):
                self.mm(s1, self.onesf, y[:, cj, :], start=(cj == 0), stop=(cj == KC - 1))
            for cj in range(KC):
                sq = self.stage([128, 512], F32)
                self.act(sq, y[:, cj, :], AF.Square)
                self.mm(s2, self.onesf, sq, start=(cj == 0), stop=(cj == KC - 1))
            mean = self.view(3, 0, [128, 512], F32, name="cmean")
            self.ts(mean, s1, 1.0 / c.CC)
            msq = self.stage([128, 512], F32)
            self.tt(msq, mean, mean, ALU.mult)
            var = self.stage([128, 512], F32)
            self.stt(var, s2, 1.0 / c.CC, msq, ALU.mult, ALU.subtract)
            rstd = self.view(3, 2048, [128, 512], F32, name="crstd")
            self.act(rstd, var, AF.Sqrt, bias=self.epsb)
            self.recip(rstd, rstd)
            for cj in range(KC):
                t1 = self.stage([128, 512], F32)
                self.tt(t1, y[:, cj, :], mean, ALU.subtract)
                self.tt(t1, t1, rstd, ALU.mult)
                ob = self.stage([128, 512], BF16)
                self.act(ob, t1, AF.Silu, scale=self.ppc(c.plg + l * KC + cj), bias=self.ppc(c.plb + l * KC + cj))
                r0 = c.AW + cj * 128
                self.dma(self.D(self.brT_d, ("b", cj, tt), (slice(r0, r0 + 128), tsl)), ob, q="act")

    def phaseHgrn1(self, l):
        c = self.c
        TPC = c.TPC
        CS = 32
        NCH = TPC // CS
        PB = 512 // CS
        self.fence([0, 1, 2, 3, 4])
        msk = self.pp[0:CS, c.pm64:c.pm64 + CS]
        for hh in range(c.HH):
            rows = slice(hh * 128, (hh + 1) * 128)
            sl = hh % 2
            V = lambda i: self.view(sl, i * 8192, [128, TPC], F32, name=f"hg{sl}_{i}")
            qs, lf, kk, cum = V(0), V(1), V(2), V(3)
            cumg = self.view(2 + sl, 0, [128, TPC], F32, name=f"hw{sl}_0")
            ec = self.view(2 + sl, 8192, [128, TPC], F32, name=f"hw{sl}_1")
            vt = self.view(2 + sl, 16384, [CS, NCH, 128], BF16, name=f"vt{sl}")
            qt = self.view(4, sl * 16384, [128, TPC], BF16, name=f"qt{sl}")
            kt = self.view(4, sl * 16384 + 4096, [128, TPC], BF16, name=f"kt{sl}")
            qg = self.view(4, sl * 16384 + 8192, [128, TPC], BF16, name=f"qg{sl}")
            qi = self.view(4, sl * 16384 + 12288, [128, TPC], BF16, name=f"qi{sl}")
            self.dma(qs, B(self.hq_d.ap()[rows, :], Res("t")), extra_r=self.Dall(self.hq_d))
            self.dma(lf, B(self.lf_d.ap()[rows, :], Res("t")), extra_r=self.Dall(self.lf_d))
            self.dma(kk, B(self.kk_d.ap()[rows, :], Res("t")), extra_r=self.Dall(self.kk_d))
            self.dma(vt, B(self.hv_d.ap()[:, rows].rearrange("(n s) d -> s n d", s=CS), Res("t")), extra_r=self.Dall(self.hv_d))
            for ch in range(NCH):
                cs = slice(ch * CS, (ch + 1) * CS)
                self.scan(cum[:, cs], self.onesb[:, 0:CS], lf[:, cs])
            self.scan(cumg, self.onesb[:, 0:TPC], lf)
            ncum = lf
            self.ts(ncum, cum, -1.0)
            for ch in range(NCH):
                cs = slice(ch * CS, (ch + 1) * CS)
                mid = ch * CS + CS // 2 - 1
                self.act(ec[:, cs], cum[:, cs], AF.Exp, bias=ncum[:, mid:mid + 1])
            self.stt(qt, qs, 128.0 ** -0.5, ec, ALU.mult, ALU.mult)
            for ch in range(NCH):
                cs = slice(ch * CS, (ch + 1) * CS)
                mid = ch * CS + CS // 2 - 1
                self.act(ec[:, cs], cum[:, cs], AF.Exp, scale=-1.0, bias=cum[:, mid:mid + 1])
            self.tt(kt, kk, ec, ALU.mult)
            self.act(ec, cum, AF.Exp)
            self.stt(qi, qs, 128.0 ** -0.5, ec, ALU.mult, ALU.mult)
            self.act(ec, cumg, AF.Exp)
            self.stt(qg, qs, 128.0 ** -0.5, ec, ALU.mult, ALU.mult)
            self.dma(self.D(self.qg_d, hh, (rows, slice(None))), qg, q="act")
            oloc = qs
            Sf = self.msc(128, 128, "Sf")
            Sbt = None
            dec = self.msc(1, 128, "dec")
            for ch in range(NCH):
                cs = slice(ch * CS, (ch + 1) * CS)
                last = ch * CS + CS - 1
                sc = self.ps([0, 1])
                self.mm(sc[0:CS, 0:CS], kt[:, cs], qt[:, cs], start=True, stop=True)
                A = self.stage([CS, CS], BF16)
                self.tt(A, sc[0:CS, 0:CS], msk, ALU.mult)
                ob = self.psum[4 + (ch // PB) % 2]
                oc = ob[:, (ch % PB) * CS:(ch % PB + 1) * CS]
                self.mm(oc, vt[:, ch, :], A, start=True, stop=(ch == 0))
                if ch > 0:
                    self.mm(oc, Sbt, qi[:, cs], start=False, stop=True)
                if ch % PB == PB - 1:
                    self.cp(oloc[:, (ch - PB + 1) * CS:(ch + 1) * CS], ob, eng="act")
                kf = self.stage([128, CS], F32)
                self.act(kf, cum[:, cs], AF.Exp, scale=-1.0, bias=cum[:, last:last + 1])
                kh = self.stage([128, CS], BF16)
                self.tt(kh, kf, kk[:, cs], ALU.mult)
                tp = self.ps([2, 3])
                tpb = B(tp.ap[0:CS, 0:64].bitcast(BF16), tp.res)
                self.tr(tpb, kh, self.ident)
                kht = self.stage([CS, 128], BF16)
                self.cp(kht, tpb, eng="act")
                sn = self.ps([6, 7])
                self.mm(sn[:, 0:128], kht, vt[:, ch, :], start=True, stop=True)
                self.act(dec, cum[:, last:last + 1], AF.Exp)
                if ch == 0:
                    self.cp(Sf, sn[:, 0:128])
                else:
                    self.stt(Sf, Sf, dec, sn[:, 0:128], ALU.mult, ALU.add)
                Sbt = self.stage([128, 128], BF16)
                self.cp(Sbt, Sf)
            self.dma(self.D(self.ol_d, hh, (rows, slice(None))), oloc, q="act")
            self.dma(self.D(self.hst_loc, ("s", hh), (rows, slice(0, 128))), Sf, q="act")
            self.dma(self.D(self.hst_loc, ("d", hh), (rows, slice(128, 129))), cumg[:, TPC - 1:TPC], q="act")
        self.ag(self.hst_all, self.hst_loc, r=self.Dall(self.hst_loc), w=[self.allres(self.hst_all)], name="hst")

    def phaseHgrn2(self, l):
        c = self.c
        TPC = c.TPC
        self.fence([0, 1, 2, 3, 4])
        NCr = c.NC
        for hh in range(c.HH):
            rows = slice(hh * 128, (hh + 1) * 128)
            sl = hh % 2
            st = self.view(sl, 0, [128, NCr, 136], F32, name=f"hst{sl}")
            self.dma(st, B(self.hst_all.ap().rearrange("(j r) n -> r j n", r=c.HW)[rows, :, :], self.allres(self.hst_all)))
            md = self.msc(8, 128, "md")
            self.tt(md.rr("p (j o) -> p j o", o=1), st[:, :, 128:129], self.pp[:, c.pm:c.pm + 8].rr("p (j o) -> p j o", o=1), ALU.mult)
            sfx = self.msc(8, 128, "sfx")
            self.memset(sfx, 0.0)
            for j in range(NCr - 2, -1, -1):
                self.tt(sfx[:, j:j + 1], sfx[:, j + 1:j + 2], md[:, j + 1:j + 2], ALU.add)
            w = self.msc(8, 128, "w8")
            self.act(w, sfx, AF.Exp)
            self.tt(w, w, self.pp[:, c.pm:c.pm + 8], ALU.mult)
            S0 = self.msc(128, 128, "S0")
            for j in range(NCr):
                if j == 0:
                    self.ts(S0, st[:, j, 0:128], w[:, j:j + 1], None, ALU.mult)
                else:
                    self.stt(S0, st[:, j, 0:128], w[:, j:j + 1], S0, ALU.mult, ALU.add)
            S0b = self.view(sl, 8192, [128, 128], BF16, name=f"S0b{sl}")
            self.cp(S0b, S0)
            qg = self.view(2 + sl, 0, [128, TPC], BF16, name=f"qg2{sl}")
            ol = self.view(2 + sl, 8192, [128, TPC], F32, name=f"ol{sl}")
            hg = self.view(2 + sl, 16384, [128, TPC], F32, name=f"hgt{sl}")
            self.dma(qg, B(self.qg_d.ap()[rows, :], Res("t")), extra_r=self.Dall(self.qg_d))
            self.dma(ol, B(self.ol_d.ap()[rows, :], Res("t")), extra_r=self.Dall(self.ol_d))
            self.dma(hg, B(self.hgt_d.ap()[rows, :], Res("t")), extra_r=self.Dall(self.hgt_d))
            for tt in range(c.NT):
                tsl = slice(tt * 512, (tt + 1) * 512)
                cp_ = self.ps([0, 1])
                self.mm(cp_, S0b, qg[:, tsl], start=True, stop=True)
                o = self.stage([128, 512], F32)
                self.tt(o, cp_, ol[:, tsl], ALU.add)
                sq = self.stage([128, 512], F32)
                self.act(sq, o, AF.Square)
                ss = self.ps([2, 3])
                self.mm(ss, self.onesf, sq, start=True, stop=True)
                rstd = self.stage([128, 512], F32)
                self.act(rstd, ss, AF.Sqrt, scale=1.0 / 128, bias=self.epsb)
                self.recip(rstd, rstd)
                self.tt(o, o, rstd, ALU.mult)
                ob = self.stage([128, 512], BF16)
                self.stt(ob, o, self.ppc(c.phg + l * c.HH + hh), hg[:, tsl], ALU.mult, ALU.mult)
                r0 = c.AW + c.CC + hh * 128
                self.dma(self.D(self.brT_d, ("c", hh, tt), (slice(r0, r0 + 128), tsl)), ob, q="act")

    def phaseC(self, l):
        c = self.c
        KD = c.KD
        self.fence([0, 1, 2, 3, 4])
        xsrc = self.xT if l == 0 else self.x1
        wi = 0
        ka, kc_, kh_ = c.AW // 128, c.CC // 128, c.HW // 128
        for tt in range(c.NT):
            tsl = slice(tt * 512, (tt + 1) * 512)
            hT = self.whole(0, [128, KD, 512], BF16)
            br = self.whole(1, [128, c.KB, 512], BF16)
            mg = self.whole(2, [128, KD, 512], BF16)
            self.dma(hT, B(self.hT_d.ap()[:, tsl].rearrange("(k p) t -> p k t", p=128), Res("t")), extra_r=self.Dall(self.hT_d))
            self.dma(br, B(self.brT_d.ap()[:, tsl].rearrange("(k p) t -> p k t", p=128), Res("t")), extra_r=self.Dall(self.brT_d))
            for fc in range(KD):
                fs = slice(fc * 128, (fc + 1) * 128)
                slot = 3 + wi % 2
                wi += 1
                o = 0
                wts = {}
                for (nm, nk, c0) in (("w_ao", ka, fc * 128), ("w_co", kc_, fc * 128), ("w_ho", kh_, fc * 128),
                                     ("w_in", KD, c.oga + fc * 128), ("w_in", KD, c.ogb + fc * 128), ("w_in", KD, c.ogc + fc * 128)):
                    v = self.view(slot, o, [128, nk, 128], BF16, name=f"wc{slot}_{o}")
                    o += nk * 256
                    wv = self.wfull[nm, l].ap().rearrange("(k p) n -> p k n", p=128)
                    self.dma(v, B(wv[:, :, c0:c0 + 128], self.wres(nm, l)))
                    wts[len(wts)] = v
                accs = []
                boff = 0
                for i, nk in enumerate((ka, kc_, kh_)):
                    psb = self.psum[i]
                    for k in range(nk):
                        self.mm(psb, wts[i][:, k, :], br[:, boff + k, :], start=(k == 0), stop=(k == nk - 1))
                    boff += nk
                    accs.append(psb)
                gs = []
                for i in range(3):
                    psb = self.psum[3 + i]
                    for k in range(KD):
                        self.mm(psb, wts[3 + i][:, k, :], hT[:, k, :], start=(k == 0), stop=(k == KD - 1))
                    sg = self.stage([128, 512], F32)
                    self.act(sg, psb, AF.Sigmoid)
                    gs.append(sg)
                m = self.stage([128, 512], F32)
                self.tt(m, accs[0], gs[0], ALU.mult)
                t2 = self.stage([128, 512], F32)
                self.tt(t2, accs[1], gs[1], ALU.mult)
                self.tt(m, m, t2, ALU.add)
                self.tt(t2, accs[2], gs[2], ALU.mult)
                self.tt(mg[:, fc, :], m, t2, ALU.add)
            self.fence([3, 4])
            ssp = self.psum[7]
            for fc in range(KD):
                slot = 3 + wi % 2
                wi += 1
                wt = self.ldw(slot, "w_o", l, [(fc * 128, 128)], KD)
                psb = self.ps([0, 1, 2, 3])
                for k in range(KD):
                    self.mm(psb, wt[:, k, :], mg[:, k, :], start=(k == 0), stop=(k == KD - 1))
                mo = self.stage([128, 512], F32)
                self.cp(mo, psb)
                self.dma(self.D(self.mo_d, (fc, tt), (slice(fc * 128, (fc + 1) * 128), tsl)), mo, q="act")
                sq = self.stage([128, 512], F32)
                self.act(sq, mo, AF.Square)
                self.mm(ssp, self.onesf, sq, start=(fc == 0), stop=(fc == KD - 1))
            rstd = self.stage([128, 512], F32)
            self.act(rstd, ssp, AF.Sqrt, scale=1.0 / c.D, bias=self.epsb)
            self.recip(rstd, rstd)
            rs1 = self.view(3, 0, [128, 512], F32, name="rs1")
            self.cp(rs1, rstd)
            ss2 = self.psum[6]
            for fc in range(KD):
                rows = slice(fc * 128, (fc + 1) * 128)
                mo = self.stage([128, 512], F32)
                self.dma(mo, self.D(self.mo_d, (fc, tt), (rows, tsl)))
                xc = self.stage([128, 512], F32)
                self.dma(xc, self.D(xsrc, (fc, tt), (rows, tsl)))
                self.stt(mo, mo, self.ppc(c.pg + (l * 4 + 1) * KD + fc), rs1, ALU.mult, ALU.mult)
                self.tt(xc, xc, mo, ALU.add)
                self.dma(self.D(self.xmid, (fc, tt), (rows, tsl)), xc, q="act")
                sq = self.stage([128, 512], F32)
                self.act(sq, xc, AF.Square)
                self.mm(ss2, self.onesf, sq, start=(fc == 0), stop=(fc == KD - 1))
            rstd2 = self.view(3, 2048, [128, 512], F32, name="rs2")
            self.act(rstd2, ss2, AF.Sqrt, scale=1.0 / c.D, bias=self.epsb)
            self.recip(rstd2, rstd2)
            for fc in range(KD):
                rows = slice(fc * 128, (fc + 1) * 128)
                xc = self.stage([128, 512], F32)
                self.dma(xc, self.D(self.xmid, (fc, tt), (rows, tsl)))
                h2 = self.stage([128, 512], BF16)
                self.stt(h2, xc, self.ppc(c.pg + (l * 4 + 2) * KD + fc), rstd2, ALU.mult, ALU.mult)
                self.dma(self.D(self.h2_d, (fc, tt), (rows, tsl)), h2, q="act")
                if tt == c.NT - 1:
                    self.dma(self.D(self.h2h_loc, fc, (rows, slice(None))), h2[:, 512 - 16:512], q="act")
            self.fence([3])
        self.ag(self.h2h_all, self.h2h_loc, r=self.Dall(self.h2h_loc), w=[self.allres(self.h2h_all)], name="h2h")

    def phaseD1(self, l):
        c = self.c
        KD, KF = c.KD, c.KF
        self.fence([0, 1, 2, 3, 4])
        hh_ = self.view(2, 0, [128, c.NC * KD, 16], BF16, name="h2hall")
        self.dma(hh_, B(self.h2h_all.ap().rearrange("(j k p) t -> p (j k) t", p=128, k=KD), self.allres(self.h2h_all)))
        hf = self.view(2, 16384, [128, KD, 16], F32, name="h2hf")
        for j in range(c.NC):
            src = hh_[:, j * KD:(j + 1) * KD, :]
            if j == 0:
                self.ts(hf, src, self.ppc(c.po + j), None, ALU.mult)
            else:
                self.stt(hf, src, self.ppc(c.po + j), hf, ALU.mult, ALU.add)
        hb = self.view(2, 16384 + KD * 64, [128, KD, 16], BF16, name="h2hb")
        self.cp(hb, hf)
        carry = self.view(2, 16384 + KD * 96, [128, 2 * KF, 2], F32, name="carry")
        wi = 0
        for tt in range(c.NT):
            tsl = slice(tt * 512, (tt + 1) * 512)
            h2 = self.whole(tt % 2, [128, KD, 512], BF16)
            self.dma(h2, B(self.h2_d.ap()[:, tsl].rearrange("(k p) t -> p k t", p=128), Res("t")), extra_r=self.Dall(self.h2_d))
            for j in range(KF):
                slot = 3 + wi % 2
                wi += 1
                wt = self.ldw(slot, "w_up", l, [(j * 128, 128), (c.DFF + j * 128, 128)], KD)
                outs = []
                for half in range(2):
                    psb = self.ps([0, 1, 2, 3])
                    for k in range(KD):
                        self.mm(psb, wt[:, k, half * 128:(half + 1) * 128], h2[:, k, :], start=(k == 0), stop=(k == KD - 1))
                    ext = self.stage([128, 514], F32)
                    ci = half * KF + j
                    if tt == 0:
                        ph = self.ps([4, 5])
                        for k in range(KD):
                            self.mm(ph[:, 0:16], wt[:, k, half * 128:(half + 1) * 128], hb[:, k, :], start=(k == 0), stop=(k == KD - 1))
                        self.cp(ext[:, 0:2], ph[:, 14:16])
                    else:
                        self.cp(ext[:, 0:2], carry[:, ci, :])
                    self.cp(ext[:, 2:514], psb, eng="act")
                    if tt < c.NT - 1:
                        self.cp(carry[:, ci, :], ext[:, 512:514])
                    wb = c.pfw + (l * 2 * KF + ci) * 3
                    a = self.stage([128, 512], F32)
                    self.ts(a, ext[:, 2:514], self.ppc(wb + 2), None, ALU.mult)
                    self.stt(a, ext[:, 1:513], self.ppc(wb + 1), a, ALU.mult, ALU.add)
                    self.stt(a, ext[:, 0:512], self.ppc(wb + 0), a, ALU.mult, ALU.add)
                    outs.append(a)
                sg = self.stage([128, 512], F32)
                self.act(sg, outs[0], AF.Silu)
                ab = self.stage([128, 512], BF16)
                self.tt(ab, sg, outs[1], ALU.mult)
                self.dma(self.D(self.act_d, (j, tt), (slice(j * 128, (j + 1) * 128), tsl)), ab, q="act")

    def phaseD2(self, l):
        c = self.c
        KD, KF = c.KD, c.KF
        self.fence([0, 1, 2, 3, 4])
        xdst = self.x1 if l == 0 else self.outT
        G = min(6, KD)
        KG = 2 if KF % 2 == 0 else 1
        wi = 0
        for tt in range(c.NT):
            tsl = slice(tt * 512, (tt + 1) * 512)
            ssp = self.psum[7]
            for g0 in range(0, KD, G):
                ng = min(G, KD - g0)
                accs = [self.psum[i] for i in range(ng)]
                for k0 in range(0, KF, KG):
                    ai = wi % 4
                    wi += 1
                    at = self.view(0, ai * 2048, [128, KG, 512], BF16, name=f"at{ai}")
                    wt = self.view(1, ai * 4096, [128, KG, G * 128], BF16, name=f"wd{ai}")
                    self.dma(at, B(self.act_d.ap()[k0 * 128:(k0 + KG) * 128, tsl].rearrange("(k p) t -> p k t", p=128), Res("t")),
                             extra_r=self.Dall(self.act_d))
                    wv = self.wfull["w_dn", l].ap().rearrange("(k p) n -> p k n", p=128)
                    self.dma(wt[:, :, 0:ng * 128], B(wv[:, k0:k0 + KG, g0 * 128:(g0 + ng) * 128], self.wres("w_dn", l)))
                    for kk in range(KG):
                        k = k0 + kk
                        for i in range(ng):
                            self.mm(accs[i], wt[:, kk, i * 128:(i + 1) * 128], at[:, kk, :], start=(k == 0), stop=(k == KF - 1))
                for i in range(ng):
                    fc = g0 + i
                    ff = self.stage([128, 512], F32)
                    self.cp(ff, accs[i])
                    self.dma(self.D(self.mo_d, (fc, tt), (slice(fc * 128, (fc + 1) * 128), tsl)), ff, q="act")
                    sq = self.stage([128, 512], F32)
                    self.act(sq, ff, AF.Square)
                    self.mm(ssp, self.onesf, sq, start=(fc == 0), stop=(fc == KD - 1))
            rstd = self.view(2, 0, [128, 512], F32, name="rs3")
            self.act(rstd, ssp, AF.Sqrt, scale=1.0 / c.D, bias=self.epsb)
            self.recip(rstd, rstd)
            for fc in range(KD):
                rows = slice(fc * 128, (fc + 1) * 128)
                ff = self.stage([128, 512], F32)
                self.dma(ff, self.D(self.mo_d, (fc, tt), (rows, tsl)))
                xc = self.stage([128, 512], F32)
                self.dma(xc, self.D(self.xmid, (fc, tt), (rows, tsl)))
                self.stt(ff, ff, self.ppc(c.pg + (l * 4 + 3) * KD + fc), rstd, ALU.mult, ALU.mult)
                self.tt(xc, xc, ff, ALU.add)
                o = self.dma(self.D(xdst, (fc, tt), (rows, tsl)), xc, q="act")
                if l == c.L - 1:
                    self.outs.append(o)
            self.fence([2])

    def build(self):
        c = self.c
        self.declare()
        self.alloc()
        self.epsb = self.msc(1)
        self.oneb = self.msc(1)
        self.memset(self.epsb, EPS)
        self.memset(self.oneb, 1.0)
        self.setup()
        allw = ["w_in", "w_ao", "w_co", "w_ho", "w_o", "w_up", "w_dn"]
        import os
        stop = int(os.environ.get("KSTOP", "99"))
        n = 0
        if stop > 0:
            self.weights_prologue(0, allw)
        for l in range(c.L):
            for ph in (self.phaseA, self.phaseHgrn1, self.phaseCsum, None, self.phaseAttn, self.phaseConv,
                       self.phaseHgrn2, self.phaseC, self.phaseD1, self.phaseD2):
                n += 1
                if n >= stop:
                    break
                if ph is None:
                    if l + 1 < c.L:
                        self.weights_prologue(l + 1, allw)
                else:
                    ph(l)
            if n >= stop:
                break
        dn = os.environ.get("KDUMP")
        if dn:
            src = getattr(self, dn)
            dbg = self.dt_("dbg", list(src.ap().shape), src.ap().dtype, "ExternalOutput")
            o = self.dma(B(dbg.ap(), Res("dbg")), B(src.ap(), Res("dsrc")), q="act", extra_r=self.Dall(src))
            self.outs.append(o)
        with self.nc.allow_non_contiguous_dma(reason="small strided scalar/halo transfers"):
            self.R.emit(self.nc, self.es, final_waits=self.outs)
        return self.nc


def host_inputs(cfg, inp):
    c = cfg
    L, NC = c.L, c.NC
    bf = ml_dtypes.bfloat16
    x = np.asarray(inp["x"], np.float32).reshape(c.S, c.D)
    xT = np.ascontiguousarray(x.T)
    wmap = dict(w_in="w_in", w_ao="w_attn_out", w_co="w_conv_out", w_ho="w_hgrn_out", w_o="w_o", w_up="w_ffn_up", w_dn="w_ffn_down")

    def fm(v, nch):
        return np.asarray(v, np.float32).reshape(nch, 128).T

    ng = np.asarray(inp["norm_gains"], np.float32)
    cols = []
    for l in range(L):
        for i in range(4):
            cols.append(fm(ng[l, i], c.KD))
    cw = np.asarray(inp["conv_dw"], np.float32)
    for l in range(L):
        for cj in range(c.KC):
            cols.append(cw[l][:, cj * 128:(cj + 1) * 128].T)
    for key in ("conv_b", "conv_ln_g", "conv_ln_b"):
        a = np.asarray(inp[key], np.float32)
        for l in range(L):
            cols.append(fm(a[l], c.KC))
    for key in ("hgrn_lb_logits", "hgrn_norm_g"):
        a = np.asarray(inp[key], np.float32)
        for l in range(L):
            cols.append(fm(a[l], c.HH))
    fw = np.asarray(inp["ffn_dw"], np.float32)
    for l in range(L):
        for ci in range(2 * c.KF):
            cols.append(fw[l][:, ci * 128:(ci + 1) * 128].T)
    base = np.concatenate(cols, axis=1)
    m64 = np.zeros((128, 64), np.float32)
    m64[:64] = np.triu(np.ones((64, 64), np.float32))
    bfg = np.ascontiguousarray(np.asarray(inp["b_fgate"], np.float32).T)
    ident = np.eye(128, dtype=np.float32)
    kk_, qq_ = np.meshgrid(np.arange(128), np.arange(128), indexing="ij")
    tri = np.where(kk_ > qq_, NEG, 0.0).astype(np.float32)
    ones = np.ones((128, 128), np.float32)
    aq = np.zeros((c.NAUG, c.TPC), np.float32)
    aq[4:8] = 1.0
    aq[8] = 1.0
    qsub = np.arange(c.TPC) // 128
    for i in range(c.NSUB):
        aq[9 + i] = np.where(qsub < i, NEG, 0.0)
    maps = []
    for r in range(NC):
        m = {}
        m["xT"] = np.ascontiguousarray(xT[:, r * c.TPC:(r + 1) * c.TPC])
        for n, key in wmap.items():
            w = np.asarray(inp[key], np.float32)
            if os.environ.get("KNOAG"):
                m[n + "_s"] = w
            else:
                rs = w.shape[1] // NC
                m[n + "_s"] = np.ascontiguousarray(w[:, r * rs:(r + 1) * rs, :])
        mlt = (np.arange(8) < r).astype(np.float32)
        oh = (np.arange(8) == r - 1).astype(np.float32)
        m["pp"] = np.ascontiguousarray(np.concatenate(
            [base, np.tile(mlt, (128, 1)), np.tile(oh, (128, 1)), m64], axis=1).astype(np.float32))
        assert m["pp"].shape[1] == c.NPP
        m["bfg"] = bfg
        own = [ident if j == r else np.zeros((128, 128), np.float32) for j in range(8)]
        m["cb"] = np.concatenate([ident, tri, ones] + own, axis=1).astype(bf)
        ak = np.zeros((c.NAUG, c.S), np.float32)
        ak[0:4] = -1.0
        seg = np.arange(c.S) // c.TPC
        ak[8] = np.where(seg > r, NEG, 0.0)
        kblk = (np.arange(c.S) % c.TPC) // 128
        for i in range(c.NSUB):
            ak[9 + i] = ((seg == r) & (kblk == i)).astype(np.float32)
        m["augKc"] = ak.astype(bf)
        m["augQc"] = aq.astype(bf)
        maps.append(m)
    return maps


_CACHE = {}


def run_cfg(cfg, inp):
    key = (cfg.D, cfg.S, cfg.HA, cfg.CC, cfg.HH, cfg.DFF)
    if key not in _CACHE:
        _CACHE[key] = Builder(cfg).build()
    nc = _CACHE[key]
    maps = host_inputs(cfg, inp)
    res = run_bass_kernel_spmd(nc, maps, core_ids=list(range(cfg.NC)))
    if os.environ.get("KDUMP"):
        return [np.asarray(res.results[r]["dbg"]) for r in range(cfg.NC)]
    outT = np.concatenate([res.results[r]["outT"] for r in range(cfg.NC)], axis=1)
    return np.ascontiguousarray(outT.T).reshape(1, cfg.S, cfg.D).astype(np.float32)


def kernel(**inputs):
    return run_cfg(Cfg(), inputs)
```
